# Optimizing a Trainium2 kernel written in Bass

```python
import jax, jax.numpy as jnp
from jax import lax
import numpy as np

D_MODEL = 2048
BATCH = 2
SEQ = 8192
DEPTH = 1

RET_HEADS = 8
RET_QK_DIM = 128
RET_V_DIM = 256
RET_QK_WIDTH = RET_HEADS * RET_QK_DIM
RET_V_WIDTH = RET_HEADS * RET_V_DIM
RET_CHUNK = 128
ROPE_THETA = 10000.0
SGU_CHUNK = 128
SGU_GROUPS = 16
SGU_WIDTH = D_MODEL
SGU_GROUP_DIM = SGU_WIDTH // SGU_GROUPS
N_BRANCH = 2
N_GROUPS = 4
EXPERTS_PER_GROUP = 8
N_EXPERTS = N_GROUPS * EXPERTS_PER_GROUP
TOP_K = 2
D_EXPERT = 512
LN_EPS = 1e-5
DN_ALPHA = (2 * DEPTH) ** 0.25
DN_BETA = (8 * DEPTH) ** -0.25

SPLIT_SIZES = (RET_QK_WIDTH, RET_QK_WIDTH, RET_V_WIDTH, RET_V_WIDTH,
               SGU_WIDTH, SGU_WIDTH, D_MODEL, D_MODEL)
SPLIT_POINTS = tuple(int(p) for p in np.cumsum(SPLIT_SIZES)[:-1])
IN_WIDTH = int(sum(SPLIT_SIZES))
COL_SCALES = (1.0, 1.0, DN_BETA, 1.0, DN_BETA, 1.0, 1.0, 1.0)

kernel_name = "hybrid_retention_sgu_hiermoe_deepnorm"


def layer_norm(x, g, b):
    xf = x.astype(jnp.float32)
    mu = jnp.mean(xf, axis=-1, keepdims=True)
    var = jnp.mean(jnp.square(xf - mu), axis=-1, keepdims=True)
    return ((xf - mu) * lax.rsqrt(var + LN_EPS) * g + b).astype(x.dtype)


def group_norm_heads(y, g):
    yf = y.astype(jnp.float32)
    mu = jnp.mean(yf, axis=-1, keepdims=True)
    var = jnp.mean(jnp.square(yf - mu), axis=-1, keepdims=True)
    yn = (yf - mu) * lax.rsqrt(var + LN_EPS) * g.reshape(y.shape[2], y.shape[3]).astype(jnp.float32)
    return yn.reshape(y.shape[0], y.shape[1], -1)


def rotary(x, positions):
    half = x.shape[-1] // 2
    freq = ROPE_THETA ** (-jnp.arange(half, dtype=jnp.float32) / half)
    ang = positions.astype(jnp.float32)[..., None] * freq
    cos = jnp.cos(ang)[:, :, None, :]
    sin = jnp.sin(ang)[:, :, None, :]
    xf = x.astype(jnp.float32)
    x1, x2 = xf[..., :half], xf[..., half:]
    return jnp.concatenate([x1 * cos - x2 * sin, x2 * cos + x1 * sin], axis=-1)


def retention(q, k, v):
    b, s, h, dk = q.shape
    dv = v.shape[-1]
    c = RET_CHUNK
    nc = s // c
    log_gamma = jnp.log1p(-jnp.exp2(-5.0 - jnp.arange(h, dtype=jnp.float32)))
    idx = jnp.arange(c, dtype=jnp.float32)
    rel = idx[:, None] - idx[None, :]
    decay_mask = jnp.where(rel >= 0,
                           jnp.exp(log_gamma[:, None, None] * jnp.maximum(rel, 0.0)),
                           0.0)
    xi = jnp.exp(log_gamma[:, None] * (idx + 1.0))
    zeta = jnp.exp(log_gamma[:, None] * (c - 1.0 - idx))
    chunk_decay = jnp.exp(log_gamma * c)

    def to_chunks(t):
        return t.astype(jnp.float32).reshape(b, nc, c, h, -1).transpose(1, 0, 3, 2, 4)

    def step(state, inp):
        qc, kc, vc = inp
        scores = jnp.einsum('bhnd,bhmd->bhnm', qc, kc) * decay_mask
        inner = jnp.einsum('bhnm,bhme->bhne', scores, vc)
        cross = jnp.einsum('bhnd,bhde->bhne', qc, state) * xi[:, :, None]
        new_state = (state * chunk_decay[:, None, None]
                     + jnp.einsum('bhmd,bhme->bhde', kc * zeta[:, :, None], vc))
        return new_state, inner + cross

    state0 = jnp.zeros((b, h, dk, dv), jnp.float32)
    _, out = lax.scan(step, state0, (to_chunks(q), to_chunks(k), to_chunks(v)))
    return out.transpose(1, 0, 3, 2, 4).reshape(b, s, h, dv)


def spatial_gating(u, v, ln_g, ln_b, w_s, b_s):
    b, s, w = v.shape
    nc = s // SGU_CHUNK
    vn = layer_norm(v, ln_g, ln_b).reshape(b, nc, SGU_CHUNK, SGU_GROUPS, SGU_GROUP_DIM)
    causal = jnp.tril(jnp.ones((SGU_CHUNK, SGU_CHUNK), dtype=w_s.dtype))
    mix = jnp.einsum('gts,bnsgd->bntgd', w_s * causal, vn)
    mix = mix + jnp.transpose(b_s)[None, None, :, :, None]
    return u * mix.reshape(b, s, w)


def hier_moe(x, w_group, b_group, w_er, b_er, w1, w3, w2):
    b, s, d = x.shape
    xt = x.reshape(b * s, d)
    g_logits = (xt @ w_group + b_group).astype(jnp.float32)
    g_prob = jax.nn.softmax(g_logits, axis=-1)
    g_idx = jnp.argmax(g_logits, axis=-1)
    p_group = jnp.take_along_axis(g_prob, g_idx[:, None], axis=1)[:, 0]
    e_logits_all = (xt @ w_er + b_er).astype(jnp.float32).reshape(-1, N_GROUPS, EXPERTS_PER_GROUP)
    e_logits = jnp.take_along_axis(e_logits_all, g_idx[:, None, None], axis=1)[:, 0]
    top_val, top_idx = lax.top_k(e_logits, TOP_K)
    p_expert = jax.nn.softmax(top_val, axis=-1)
    expert_ids = g_idx[:, None] * EXPERTS_PER_GROUP + top_idx
    combine = p_group[:, None] * jnp.sum(
        jax.nn.one_hot(expert_ids, N_EXPERTS, dtype=jnp.float32) * p_expert[..., None], axis=1)
    combine = combine.astype(x.dtype)
    y = jnp.zeros_like(xt)
    for e in range(N_EXPERTS):
        hdn = jax.nn.silu(xt @ w1[e]) * (xt @ w3[e])
        y = y + combine[:, e:e + 1] * (hdn @ w2[e])
    return y.reshape(b, s, d)


def setup_inputs(seed: int = 0) -> dict:
    key = jax.random.key(seed)
    ks = jax.random.split(key, 24)
    f32 = jnp.float32
    D = D_MODEL
    nrm = lambda k, shp: jax.random.normal(k, shp, f32)
    col_scale = jnp.concatenate([jnp.full((sz,), sc, f32) for sz, sc in zip(SPLIT_SIZES, COL_SCALES)])
    inputs = {
        "x": nrm(ks[0], (BATCH, SEQ, D)),
        "positions": jnp.broadcast_to(jnp.arange(SEQ, dtype=jnp.int32), (BATCH, SEQ)),
        "w_in": nrm(ks[1], (DEPTH, D, IN_WIDTH)) * (D ** -0.5) * col_scale,
        "b_gate": 0.1 * nrm(ks[2], (DEPTH, N_BRANCH, D)),
        "ret_gn_g": 1.0 + 0.1 * nrm(ks[3], (DEPTH, RET_V_WIDTH)),
        "sgu_ln_g": 1.0 + 0.1 * nrm(ks[4], (DEPTH, SGU_WIDTH)),
        "sgu_ln_b": 0.02 * nrm(ks[5], (DEPTH, SGU_WIDTH)),
        "sgu_w": nrm(ks[6], (DEPTH, SGU_GROUPS, SGU_CHUNK, SGU_CHUNK)) * (SGU_CHUNK ** -0.5),
        "sgu_b": 1.0 + 0.1 * nrm(ks[7], (DEPTH, SGU_GROUPS, SGU_CHUNK)),
        "w_proj_ret": nrm(ks[8], (DEPTH, RET_V_WIDTH, D)) * (RET_V_WIDTH ** -0.5) * DN_BETA,
        "w_proj_sgu": nrm(ks[9], (DEPTH, SGU_WIDTH, D)) * (SGU_WIDTH ** -0.5) * DN_BETA,
        "w_out": nrm(ks[10], (DEPTH, D, D)) * (D ** -0.5) * DN_BETA,
        "ln1_g": 1.0 + 0.1 * nrm(ks[11], (DEPTH, D)),
        "ln1_b": 0.02 * nrm(ks[12], (DEPTH, D)),
        "w_group": nrm(ks[13], (DEPTH, D, N_GROUPS)) * (D ** -0.5),
        "b_group": 0.01 * nrm(ks[14], (DEPTH, N_GROUPS)),
        "w_er": nrm(ks[15], (DEPTH, D, N_EXPERTS)) * (D ** -0.5),
        "b_er": 0.01 * nrm(ks[16], (DEPTH, N_EXPERTS)),
        "w1": nrm(ks[17], (DEPTH, N_EXPERTS, D, D_EXPERT)) * (D ** -0.5) * DN_BETA,
        "w3": nrm(ks[18], (DEPTH, N_EXPERTS, D, D_EXPERT)) * (D ** -0.5) * DN_BETA,
        "w2": nrm(ks[19], (DEPTH, N_EXPERTS, D_EXPERT, D)) * (D_EXPERT ** -0.5) * DN_BETA,
        "ln2_g": 1.0 + 0.1 * nrm(ks[20], (DEPTH, D)),
        "ln2_b": 0.02 * nrm(ks[21], (DEPTH, D)),
    }
    return inputs


def reference(x, positions, w_in, b_gate, ret_gn_g, sgu_ln_g, sgu_ln_b, sgu_w, sgu_b,
              w_proj_ret, w_proj_sgu, w_out, ln1_g, ln1_b, w_group, b_group, w_er, b_er,
              w1, w3, w2, ln2_g, ln2_b):
    b, s, _ = x.shape
    for l in range(DEPTH):
        h = x @ w_in[l]
        q, k, v, g_ret, u, vs, g_a, g_b = jnp.split(h, SPLIT_POINTS, axis=-1)
        q = rotary(q.reshape(b, s, RET_HEADS, RET_QK_DIM), positions)
        k = rotary(k.reshape(b, s, RET_HEADS, RET_QK_DIM), positions) * (RET_QK_DIM ** -0.5)
        v = v.reshape(b, s, RET_HEADS, RET_V_DIM)
        y_ret = group_norm_heads(retention(q, k, v), ret_gn_g[l]).astype(x.dtype)
        y_a = (jax.nn.silu(g_ret) * y_ret) @ w_proj_ret[l]
        y_sgu = spatial_gating(jax.nn.gelu(u, approximate=False), jax.nn.gelu(vs, approximate=False),
                               sgu_ln_g[l], sgu_ln_b[l], sgu_w[l], sgu_b[l])
        y_b = y_sgu @ w_proj_sgu[l]
        merged = (jax.nn.sigmoid(g_a + b_gate[l, 0]) * y_a
                  + jax.nn.sigmoid(g_b + b_gate[l, 1]) * y_b)
        x = layer_norm(DN_ALPHA * x + merged @ w_out[l], ln1_g[l], ln1_b[l])
        moe = hier_moe(x, w_group[l], b_group[l], w_er[l], b_er[l], w1[l], w3[l], w2[l])
        x = layer_norm(DN_ALPHA * x + moe, ln2_g[l], ln2_b[l])
    return x
```

```python
import math
from contextlib import ExitStack

import numpy as np
import concourse.bass as bass
import concourse.mybir as mybir
from concourse.bass_utils import run_bass_kernel_spmd

F32 = mybir.dt.float32
BF16 = mybir.dt.bfloat16
I32 = mybir.dt.int32
U8 = mybir.dt.uint8
AF = mybir.ActivationFunctionType
OP = mybir.AluOpType

D = 2048
KT = 16
H = 8
DK = 128
DV = 256
C = 128
NE = 32
DE = 512
IN_W = 14336
LN_EPS = 1e-5
DN_ALPHA = 2.0 ** 0.25
OFF_Q, OFF_K, OFF_V, OFF_GR, OFF_U, OFF_VS, OFF_GA, OFF_GB = 0, 1024, 2048, 4096, 6144, 8192, 10240, 12288
MAGIC = 12582912.0
TWO_PI_HI = 6.28125
TWO_PI_LO = 2.0 * math.pi - 6.28125
PI_SAFE = 3.1415925
SAME_ENG_SYNC = True


class Buf:
    __slots__ = ("name", "w", "r", "sem", "n", "excl")

    def __init__(self, name, excl=False):
        self.name = name
        self.excl = excl
        self.w = None
        self.r = {}
        self.sem = None
        self.n = 0


class Kern:
    def __init__(self, nc, es):
        self.nc = nc
        self.es = es
        self.E = {"pe": nc.tensor, "act": nc.scalar, "dve": nc.vector, "pool": nc.gpsimd, "sp": nc.sync}
        self.sem = {e: es.enter_context(nc.semaphore("s_" + e)) for e in self.E}
        self.cnt = {e: 0 for e in self.E}
        self.seen = {e: {} for e in self.E}
        self.dmasems = []
        self.nsem = 0

    def _wait(self, eng, deps):
        need = {}
        for sem, val in deps:
            cur = need.get(sem.num)
            if cur is None or cur[1] < val:
                need[sem.num] = (sem, val)
        for num, (sem, val) in need.items():
            if self.seen[eng].get(num, 0) >= val:
                continue
            if sem is self.sem[eng] and (eng == "pe" or not SAME_ENG_SYNC):
                continue
            self.E[eng].wait_ge(sem, val)
            self.seen[eng][num] = val

    @staticmethod
    def _deps(R, W):
        deps = []
        for b in R:
            if b.w is not None:
                deps.append(b.w)
        for b in W:
            if b.w is not None:
                deps.append(b.w)
            deps.extend(b.r.values())
        return deps

    @staticmethod
    def _mark(tok, R, W):
        for b in R:
            cur = b.r.get(tok[0].num)
            if cur is None or cur[1] < tok[1]:
                b.r[tok[0].num] = tok
        for b in W:
            b.w = tok
            b.r = {}

    def op(self, eng, fn, R=(), W=(), inc=True):
        if any(b.excl for b in R):
            W = list(W) + [b for b in R if b.excl]
            R = [b for b in R if not b.excl]
        self._wait(eng, self._deps(R, W))
        ins = fn()
        tok = (self.sem[eng], self.cnt[eng] + 1)
        if inc:
            ins.then_inc(self.sem[eng], 1)
            self.cnt[eng] += 1
        self._mark(tok, R, W)
        return tok

    def _dsem(self, q, sembuf):
        kind = "sw" if q == "pool" else "hw"
        if sembuf.sem is None:
            sembuf.sem = {}
        if kind not in sembuf.sem:
            sm = self.es.enter_context(self.nc.semaphore("d%d" % self.nsem))
            self.nsem += 1
            sembuf.sem[kind] = [sm, 0]
            self.dmasems.append(sembuf.sem[kind])
        ent = sembuf.sem[kind]
        ent[1] += 1
        return ent[0], 16 * ent[1]

    def dma(self, q, out, in_, sembuf, R=(), W=()):
        self._wait(q, self._deps(R, W))
        sm, val = self._dsem(q, sembuf)
        self.E[q].dma_start(out=out, in_=in_).then_inc(sm, 16)
        tok = (sm, val)
        self._mark(tok, R, W)
        return tok

    def dma_custom(self, q, fn, sembuf, R=(), W=()):
        self._wait(q, self._deps(R, W))
        sm, val = self._dsem(q, sembuf)
        fn().then_inc(sm, 16)
        tok = (sm, val)
        self._mark(tok, R, W)
        return tok

    def barrier(self, engines=("pe", "act", "dve", "pool", "sp")):
        toks = [(self.sem[e], self.cnt[e]) for e in self.E if self.cnt[e] > 0]
        toks += [(e[0], 16 * e[1]) for e in self.dmasems]
        for e in engines:
            self._wait(e, toks)


class StopBuild(Exception):
    pass


class CFG:
    def __init__(self, tpc=2048, npre=48, nch=4, pass_tok=1024, ne=NE, stop=99, sparse=True):
        self.stop = stop
        self.sparse = sparse
        self.TPC = tpc
        self.NCHUNK = tpc // C
        self.NPRE = npre
        self.NCH = nch
        self.T = nch * C
        self.NBLK = tpc // self.T
        self.PT = pass_tok
        self.NPASS = tpc // pass_tok
        self.NE = ne


def _cpk_layout():
    names = [("ident", 128), ("maskT", 1024), ("XI", 1024), ("ZSR", 1024), ("GC", 8), ("freq4", 256),
             ("bgate", 32), ("gng", 16), ("sgug", 16), ("sgub", 16), ("wr", KT * 36), ("brrep", 36),
             ("trilT", 128), ("tgrid", 64), ("tokgrid", 16)]
    lay = {}
    o = 0
    for n, w in names:
        lay[n] = (o, w)
        o += w
    return lay, o


CPK, CPK_W = _cpk_layout()


def build(cfg):
    nc = bass.Bass("TRN2", target_bir_lowering=False)
    TPC, NPRE, NCH, T = cfg.TPC, cfg.NPRE, cfg.NCH, cfg.T
    NCHUNK = cfg.NCHUNK

    def din(name, shape, dt=F32):
        return nc.dram_tensor(name, list(shape), dt, kind="ExternalInput").ap()

    x_loc = din("x_loc", [TPC, D])
    x_pre = din("x_pre", [max(NPRE, 1) * C, D])
    pos_loc = din("pos_loc", [128, NCHUNK], I32)
    pos_pre = din("pos_pre", [128, max(NPRE, 1)], I32)
    cpk_d = din("cpk", [128, CPK_W])
    wsT_d = din("wsT", [128, 16 * 128])
    bsrep_d = din("bsrep", [128, 16 * 128])
    ln1g_d = din("ln1g", [128, D])
    ln1b_d = din("ln1b", [128, D])
    ln2g_d = din("ln2g", [128, D])
    ln2b_d = din("ln2b", [128, D])
    w_in = din("w_in", [D, IN_W])
    w_pr = din("w_proj_ret", [D, D])
    w_ps = din("w_proj_sgu", [D, D])
    w_out = din("w_out", [D, D])
    w1 = din("w1", [NE, D, DE])
    w3 = din("w3", [NE, D, DE])
    w2 = din("w2", [NE, DE, D])
    w1r = din("w1r", [NE * 128, 8192])
    w3r = din("w3r", [NE * 128, 8192])
    w2r = din("w2r", [NE * 128, 8192])
    y_out = nc.dram_tensor("y", [TPC, D], F32, kind="ExternalOutput").ap()
    X1 = nc.dram_tensor("x1_scr", [TPC, D], F32, kind="Internal").ap()
    X1T = nc.dram_tensor("x1t_scr", [128, KT, TPC], BF16, kind="Internal").ap()
    YS = nc.dram_tensor("ys_scr", [12288, D], F32, kind="Internal").ap()
    LIST = nc.dram_tensor("list_scr", [12288, 2], I32, kind="Internal").ap()
    b_X1 = Buf("X1")
    b_X1T = Buf("X1T")

    es = ExitStack()
    try:
        _build_body(nc, es, cfg, locals())
    except StopBuild:
        pass
    return nc


def _build_body(nc, es, cfg, L):
    globals_ = L
    (TPC, NPRE, NCH, T, NCHUNK) = (L['TPC'], L['NPRE'], L['NCH'], L['T'], L['NCHUNK'])
    x_loc, x_pre, pos_loc, pos_pre, cpk_d, wsT_d, bsrep_d = (L[k] for k in ('x_loc','x_pre','pos_loc','pos_pre','cpk_d','wsT_d','bsrep_d'))
    ln1g_d, ln1b_d, ln2g_d, ln2b_d, w_in, w_pr, w_ps, w_out, w1, w3, w2 = (L[k] for k in ('ln1g_d','ln1b_d','ln2g_d','ln2b_d','w_in','w_pr','w_ps','w_out','w1','w3','w2'))
    w1r, w3r, w2r = L['w1r'], L['w3r'], L['w2r']
    y_out, X1, X1T, b_X1, b_X1T, YS, LIST = (L[k] for k in ('y_out','X1','X1T','b_X1','b_X1T','YS','LIST'))
    with es:
        K = Kern(nc, es)

        def chk(n):
            if cfg.stop == n:
                K.barrier()
                raise StopBuild()
        RING_N = 3
        SB_TOTAL = 207 * 1024
        CP_B = 17664
        KEEP_B = 27 * 1024
        RING_B = RING_N * 16384
        ARENA_B = SB_TOTAL - CP_B - KEEP_B - RING_B
        sb = es.enter_context(nc.sbuf_tensor("sb", [128, SB_TOTAL], U8))
        cp = sb[:, 0:CPK_W * 4].bitcast(F32)
        keep = sb[:, CP_B:CP_B + KEEP_B]
        ringb = sb[:, CP_B + KEEP_B:CP_B + KEEP_B + RING_B]
        ring_v = [ringb[:, i * 16384:(i + 1) * 16384].bitcast(BF16) for i in range(RING_N)]
        arena = sb[:, CP_B + KEEP_B + RING_B:SB_TOTAL]
        ps = es.enter_context(nc.psum_tensor("ps", [128, 8, 512], F32))
        b_cp = Buf("cp")
        ring_b = [Buf("ring%d" % i) for i in range(RING_N)]
        bank_b = [Buf("bank%d" % i, excl=True) for i in range(8)]
        st = {"bank": 0}

        def pb():
            i = st["bank"] % 8
            st["bank"] += 1
            return bank_b[i], ps[:, i, :]

        def carve(base, off, shape, dt):
            esz = {F32: 4, BF16: 2, I32: 4, U8: 1}[dt]
            n = 1
            for s in shape[1:]:
                n *= s
            ap = base[:, off:off + n * esz]
            if dt != U8:
                ap = ap.bitcast(dt)
            if len(shape) == 3:
                ap = ap.rearrange("p (a b) -> p a b", a=shape[1])
            elif len(shape) == 4:
                ap = ap.rearrange("p (a b c) -> p a b c", a=shape[1], b=shape[2])
            return ap, off + n * esz

        def cpv(name):
            o, w = CPK[name]
            return cp[:, o:o + w]

        K.dma("sp", cp, cpk_d, b_cp, W=[b_cp])
        ident = cpv("ident")
        maskT = cpv("maskT").rearrange("p (h n) -> p h n", h=H)
        XI = cpv("XI").rearrange("p (h n) -> p h n", h=H)
        ZSR = cpv("ZSR").rearrange("p (h n) -> p h n", h=H)
        GC = cpv("GC")
        freq4 = cpv("freq4")
        bgate = cpv("bgate").rearrange("p (a b) -> p a b", a=2)
        gng = cpv("gng")
        sgug = cpv("sgug")
        sgub = cpv("sgub")
        wr = cpv("wr").rearrange("p (k c) -> p k c", k=KT)
        brrep = cpv("brrep")
        trilT = cpv("trilT")

        ko = 0
        S32, ko = carve(keep, ko, [128, H * DV], F32)
        Sbf, ko = carve(keep, ko, [128, H * DV], BF16)
        WsT, ko = carve(keep, ko, [128, 16, 128], BF16)
        B2, ko = carve(keep, ko, [128, 16, 128], F32)
        comb, ko = carve(keep, ko, [128, NCHUNK, NE], F32)
        ones32, ko = carve(keep, ko, [128, 128], F32)
        assert ko <= KEEP_B, ko
        b_S32 = [Buf("S32_%d" % h) for h in range(H)]
        b_Sbf = [Buf("Sbf_%d" % h) for h in range(H)]
        b_WsT, b_B2, b_ones = Buf("WsT"), Buf("B2"), Buf("ones")
        b_comb = [Buf("comb%d" % i) for i in range(NCHUNK)]

        slabs = []
        ringall = sb[:, CP_B + KEEP_B:CP_B + KEEP_B + 4 * 16384]
        ring_b.append(Buf("ring3"))

        def slot_ap(slot):
            return ringall[:, slot * 16384:(slot + 1) * 16384].bitcast(BF16)

        def v_k512(slot):
            return slot_ap(slot).rearrange("p (k c) -> p k c", k=KT)

        def v_k4(slot):
            return slot_ap(slot).rearrange("p (k c) -> p k c", k=4)

        def add_cols(wd, c0, width=512):
            src = wd[:, c0:c0 + width].rearrange("(k p) c -> p k c", p=128)
            slabs.append(dict(parts=[(lambda s: v_k512(s), src)]))

        def add_qk(hp):
            srcq = w_in[:, OFF_Q + hp * 256:OFF_Q + hp * 256 + 256].rearrange("(k p) c -> p k c", p=128)
            srck = w_in[:, OFF_K + hp * 256:OFF_K + hp * 256 + 256].rearrange("(k p) c -> p k c", p=128)
            slabs.append(dict(parts=[(lambda s: v_k512(s)[:, :, 0:256], srcq), (lambda s: v_k512(s)[:, :, 256:512], srck)]))

        for blk in range(cfg.NBLK):
            for hp in range(4):
                add_qk(hp)
                add_cols(w_in, OFF_V + hp * 512)
                add_cols(w_in, OFF_GR + hp * 512)
            for j in range(4):
                add_cols(w_in, OFF_VS + j * 512)
            for j in range(4):
                add_cols(w_in, OFF_U + j * 512)
            for j in range(4):
                add_cols(w_in, OFF_GA + j * 512)
                add_cols(w_in, OFF_GB + j * 512)
                add_cols(w_pr, j * 512)
                add_cols(w_ps, j * 512)
            for j in range(4):
                add_cols(w_out, j * 512)
        NSLAB_B = len(slabs)
        for i, sd in enumerate(slabs):
            sd["slot"] = i % 3
        for p_ in range(0 if cfg.sparse else cfg.NPASS):
            for e in range(cfg.NE):
                slabs.append(dict(parts=[(lambda s: v_k512(s), w1[e].rearrange("(k p) c -> p k c", p=128))]))
                slabs.append(dict(parts=[(lambda s: v_k512(s), w3[e].rearrange("(k p) c -> p k c", p=128))]))
                slabs.append(dict(parts=[(lambda s: v_k4(s), w2[e].rearrange("(k p) c -> p k c", p=128))]))
        for i in range(NSLAB_B, len(slabs)):
            slabs[i]["slot"] = (i - NSLAB_B) % 4
        lastin = {}
        for i, sd in enumerate(slabs):
            sd["prev"] = lastin.get(sd["slot"])
            lastin[sd["slot"]] = i
        sl = {"issued": 0, "next": 0, "limit": NSLAB_B, "rel": set()}

        def try_issue():
            while sl["issued"] < sl["limit"]:
                sd = slabs[sl["issued"]]
                if sd["prev"] is not None and sd["prev"] not in sl["rel"]:
                    break
                for vf, src in sd["parts"]:
                    K.dma("pool", vf(sd["slot"]), src, ring_b[sd["slot"]], W=[ring_b[sd["slot"]]])
                sl["issued"] += 1

        def acquire():
            i = sl["next"]
            sl["next"] += 1
            try_issue()
            assert sl["issued"] > i, (i, sl["issued"])
            return i, ring_b[slabs[i]["slot"]], slabs[i]["slot"]

        def release(i):
            sl["rel"].add(i)
            try_issue()

        def acquire_b():
            if sl["next"] > 0:
                sl["rel"].add(sl["next"] - 1)
            return acquire()

        def rstd_from_var(var_ap, out_ap, tmp_ap, bufs_r, bufs_w):
            K.op("act", lambda: nc.scalar.activation(out=tmp_ap, in_=var_ap, func=AF.Sqrt, bias=eps_col[:, 0:1], scale=1.0),
                 R=bufs_r, W=bufs_w)
            K.op("dve", lambda: nc.vector.reciprocal(out=out_ap, in_=tmp_ap), R=bufs_w, W=bufs_w)

        def trig_chunk(posf_col, ang, tmp, sin_o, cos_o, b_pos, b_t):
            K.op("dve", lambda: nc.vector.tensor_scalar(out=ang, in0=freq4, scalar1=posf_col, scalar2=None, op0=OP.mult),
                 R=[b_pos, b_cp], W=[b_t])
            K.op("dve", lambda: nc.vector.tensor_scalar(out=tmp, in0=ang, scalar1=1.0 / (2.0 * math.pi), scalar2=MAGIC,
                                                        op0=OP.mult, op1=OP.add), R=[b_t], W=[b_t])
            K.op("dve", lambda: nc.vector.tensor_scalar(out=tmp, in0=tmp, scalar1=-MAGIC, scalar2=None, op0=OP.add),
                 R=[b_t], W=[b_t])
            K.op("dve", lambda: nc.vector.scalar_tensor_tensor(out=ang, in0=tmp, scalar=-TWO_PI_HI, in1=ang,
                                                               op0=OP.mult, op1=OP.add), R=[b_t], W=[b_t])
            K.op("dve", lambda: nc.vector.scalar_tensor_tensor(out=ang, in0=tmp, scalar=-TWO_PI_LO, in1=ang,
                                                               op0=OP.mult, op1=OP.add), R=[b_t], W=[b_t])
            K.op("dve", lambda: nc.vector.tensor_scalar(out=ang, in0=ang, scalar1=-PI_SAFE, scalar2=PI_SAFE,
                                                        op0=OP.max, op1=OP.min), R=[b_t], W=[b_t])
            K.op("dve", lambda: nc.vector.scalar_tensor_tensor(out=tmp, in0=ang, scalar=-1.0, in1=ang, op0=OP.mult, op1=OP.max),
                 R=[b_t], W=[b_t])
            K.op("act", lambda: nc.scalar.activation(out=sin_o, in_=ang, func=AF.Sin), R=[b_t], W=[b_t])
            K.op("act", lambda: nc.scalar.activation(out=cos_o, in_=tmp, func=AF.Sin, bias=halfpi_col[:, 0:1], scale=-1.0),
                 R=[b_t], W=[b_t])

        def rotary(bank_ap, b_bank, cos4, sin4, b_t, out4, b_out, t1, t2, b_tmp):
            xv = bank_ap.rearrange("p (h t d) -> p h t d", h=4, t=2)
            ov = out4.rearrange("p h (t d) -> p h t d", t=2)
            c4 = cos4.rearrange("p (h d) -> p h d", h=4)
            s4 = sin4.rearrange("p (h d) -> p h d", h=4)
            x1, x2 = xv[:, :, 0, :], xv[:, :, 1, :]
            K.op("dve", lambda: nc.vector.tensor_tensor(out=t1, in0=x1, in1=c4, op=OP.mult), R=[b_bank, b_t], W=[b_tmp])
            K.op("dve", lambda: nc.vector.tensor_tensor(out=t2, in0=x2, in1=s4, op=OP.mult), R=[b_bank, b_t], W=[b_tmp])
            K.op("dve", lambda: nc.vector.tensor_tensor(out=ov[:, :, 0, :], in0=t1, in1=t2, op=OP.subtract),
                 R=[b_tmp], W=[b_out])
            K.op("dve", lambda: nc.vector.tensor_tensor(out=t1, in0=x2, in1=c4, op=OP.mult), R=[b_bank, b_t], W=[b_tmp])
            K.op("dve", lambda: nc.vector.tensor_tensor(out=t2, in0=x1, in1=s4, op=OP.mult), R=[b_bank, b_t], W=[b_tmp])
            K.op("dve", lambda: nc.vector.tensor_tensor(out=ov[:, :, 1, :], in0=t1, in1=t2, op=OP.add),
                 R=[b_tmp], W=[b_out])

        def transpose_to(src_ap, b_src, ntile, evac):
            for t0 in range(0, ntile, 4):
                n = min(4, ntile - t0)
                bb, bap = pb()
                for j in range(n):
                    K.op("pe", lambda j=j: nc.tensor.transpose(out=bap[:, j * 128:(j + 1) * 128],
                                                               in_=src_ap[:, (t0 + j) * 128:(t0 + j + 1) * 128],
                                                               identity=ident),
                         R=[b_src, b_cp], W=[bb], inc=(j == n - 1))
                evac(bap, bb, t0, n)

        def mm_group(bap, bb, pairs, R):
            n = len(pairs)
            for i, (l, r) in enumerate(pairs):
                K.op("pe", lambda l=l, r=r, i=i: nc.tensor.matmul(bap, lhsT=l, rhs=r, start=(i == 0), stop=(i == n - 1)),
                     R=R, W=[bb], inc=(i == n - 1))

        def layer_stats(src_chunks, b_src, mv_ap, stats_ap, b_stat):
            for j in range(4):
                K.op("dve", lambda j=j: nc.vector.bn_stats(out=stats_ap[:, j, :], in_=src_chunks[:, j * 512:(j + 1) * 512]),
                     R=[b_src], W=[b_stat])
            K.op("dve", lambda: nc.vector.bn_aggr(out=mv_ap, in_=stats_ap.rearrange("p a b -> p (a b)")), R=[b_stat], W=[b_stat])

        eps_col, ko2 = carve(keep, ko, [128, 1], F32)
        halfpi_col, ko2 = carve(keep, ko2, [128, 1], F32)
        assert ko2 <= KEEP_B
        b_cc = Buf("cc")
        K.op("dve", lambda: nc.vector.memset(eps_col, LN_EPS), W=[b_cc])
        K.op("dve", lambda: nc.vector.memset(halfpi_col, math.pi / 2.0), W=[b_cc])
        K.op("dve", lambda: nc.vector.memset(ones32, 1.0), W=[b_ones])
        for h in range(H):
            K.op("dve", lambda h=h: nc.vector.memset(S32[:, h * DV:(h + 1) * DV], 0.0), W=[b_S32[h]])

        ao = 0
        wsraw, ao = carve(arena, ao, [128, 16, 128], F32)
        bsrep, ao = carve(arena, ao, [128, 16, 128], F32)
        b_wsraw, b_bsrep = Buf("wsraw"), Buf("bsrep")
        K.dma("sp", wsraw, wsT_d.rearrange("p (g t) -> p g t", g=16), b_wsraw, W=[b_wsraw])
        K.dma("sp", bsrep, bsrep_d.rearrange("p (g t) -> p g t", g=16), b_bsrep, W=[b_bsrep])
        for g in range(16):
            K.op("dve", lambda g=g: nc.vector.tensor_tensor(out=wsraw[:, g, :], in0=wsraw[:, g, :], in1=trilT, op=OP.mult),
                 R=[b_cp], W=[b_wsraw])
        K.op("act", lambda: nc.scalar.copy(out=WsT, in_=wsraw), R=[b_wsraw], W=[b_WsT])
        for g in range(16):
            bb, bap = pb()
            K.op("pe", lambda g=g: nc.tensor.matmul(bap[:, 0:128], lhsT=ones32, rhs=wsraw[:, g, :], start=True, stop=True),
                 R=[b_ones, b_wsraw], W=[bb])
            K.op("dve", lambda g=g: nc.vector.scalar_tensor_tensor(out=B2[:, g, :], in0=bap[:, 0:128], scalar=sgub[:, g:g + 1],
                                                                   in1=bsrep[:, g, :], op0=OP.mult, op1=OP.add),
                 R=[bb, b_bsrep, b_cp], W=[b_B2])
        K.barrier()

        chk(0)
        if NPRE > 0:
            ao = 0
            wkv, ao = carve(arena, ao, [128, KT, 3072], BF16)
            assert ao <= ARENA_B
            b_wkv = Buf("wkv")
            K.dma("pool", wkv[:, :, 0:1024], w_in[:, OFF_K:OFF_K + 1024].rearrange("(k p) c -> p k c", p=128), b_wkv, W=[b_wkv])
            for j in range(4):
                K.dma("pool", wkv[:, :, 1024 + j * 512:1024 + (j + 1) * 512],
                      w_in[:, OFF_V + j * 512:OFF_V + (j + 1) * 512].rearrange("(k p) c -> p k c", p=128), b_wkv, W=[b_wkv])
            rbytes = ringb
            ro = 0
            xin2 = []
            for i in range(2):
                a_, ro = carve(rbytes, ro, [128, D], F32)
                xin2.append(a_)
            xTa = []
            for i in range(2):
                a_, ro = carve(rbytes, ro, [128, KT, 128], BF16)
                xTa.append(a_)
            pposi, ro = carve(rbytes, ro, [128, NPRE], I32)
            pposf, ro = carve(rbytes, ro, [128, NPRE], F32)
            tg = []
            for i in range(2):
                a1, ro = carve(rbytes, ro, [128, 256], F32)
                a2, ro = carve(rbytes, ro, [128, 256], F32)
                a3, ro = carve(rbytes, ro, [128, 256], F32)
                a4, ro = carve(rbytes, ro, [128, 256], F32)
                tg.append((a1, a2, a3, a4))
            krot, ro = carve(rbytes, ro, [128, 8, 128], F32)
            rt1, ro = carve(rbytes, ro, [128, 4, 64], F32)
            rt2, ro = carve(rbytes, ro, [128, 4, 64], F32)
            kz, ro = carve(rbytes, ro, [128, 8, 128], BF16)
            vbf, ro = carve(rbytes, ro, [128, H * DV], BF16)
            assert ro <= RING_N * 16384, ro
            b_xin = [Buf("xinA%d" % i) for i in range(2)]
            b_xT = [Buf("xTA%d" % i) for i in range(2)]
            b_pp = Buf("ppos")
            b_tg = [Buf("tgA%d" % i) for i in range(2)]
            b_krot, b_rt, b_kz, b_v = Buf("krot"), Buf("rt"), Buf("kz"), Buf("vA")
            K.dma("sp", pposi, pos_pre, b_pp, W=[b_pp])
            K.op("dve", lambda: nc.vector.tensor_copy(out=pposf, in_=pposi), R=[b_pp], W=[b_pp])
            for i in range(NPRE):
                xi_, bx = xin2[i % 2], b_xin[i % 2]
                xt_, bxt = xTa[i % 2], b_xT[i % 2]
                ang, tmp, sn, cs = tg[i % 2]
                btg = b_tg[i % 2]
                K.dma("sp", xi_, x_pre[i * C:(i + 1) * C, :], bx, W=[bx])
                trig_chunk(pposf[:, i:i + 1], ang, tmp, sn, cs, b_pp, btg)

                def ev_x(bap, bb, t0, n, xt_=xt_, bxt=bxt):
                    K.op("act", lambda: nc.scalar.copy(out=xt_[:, t0:t0 + n, :], in_=bap.rearrange("p (a b) -> p a b", a=4)),
                         R=[bb], W=[bxt])
                transpose_to(xi_, bx, KT, ev_x)
                kb = []
                for s_ in range(6):
                    bb, bap = pb()
                    mm_group(bap, bb, [(xt_[:, kt, :], wkv[:, kt, s_ * 512:(s_ + 1) * 512]) for kt in range(KT)], R=[bxt, b_wkv])
                    if s_ < 2:
                        rotary(bap, bb, cs, sn, btg, krot[:, s_ * 4:(s_ + 1) * 4, :], b_krot, rt1, rt2, b_rt)
                    else:
                        j = s_ - 2
                        K.op("act", lambda j=j, bap=bap: nc.scalar.copy(out=vbf[:, j * 512:(j + 1) * 512], in_=bap), R=[bb], W=[b_v])
                K.op("dve", lambda: nc.vector.tensor_tensor(out=kz, in0=krot, in1=ZSR, op=OP.mult), R=[b_krot, b_cp], W=[b_kz])
                for hp in range(4):
                    bb, bap = pb()
                    for hh in range(2):
                        h = hp * 2 + hh
                        K.op("pe", lambda h=h, hh=hh, bap=bap: nc.tensor.matmul(bap[:, hh * DV:(hh + 1) * DV], lhsT=kz[:, h, :],
                                                                                rhs=vbf[:, h * DV:(h + 1) * DV], start=True, stop=True),
                             R=[b_kz, b_v], W=[bb], inc=(hh == 1))
                    for hh in range(2):
                        h = hp * 2 + hh
                        K.op("dve", lambda h=h, hh=hh, bap=bap: nc.vector.scalar_tensor_tensor(
                            out=S32[:, h * DV:(h + 1) * DV], in0=S32[:, h * DV:(h + 1) * DV], scalar=GC[:, h:h + 1],
                            in1=bap[:, hh * DV:(hh + 1) * DV], op0=OP.mult, op1=OP.add), R=[bb, b_cp], W=[b_S32[h]])
            K.barrier()
        for h in range(H):
            K.op("act", lambda h=h: nc.scalar.copy(out=Sbf[:, h * DV:(h + 1) * DV], in_=S32[:, h * DV:(h + 1) * DV]),
                 R=[b_S32[h]], W=[b_Sbf[h]])

        chk(1)
        bo = 0
        xT, bo = carve(arena, bo, [128, KT, T], BF16)
        posi, bo = carve(arena, bo, [128, NCHUNK], I32)
        posf, bo = carve(arena, bo, [128, NCHUNK], F32)
        gst, bo = carve(arena, bo, [128, 4, 6], F32)
        AB0 = bo
        b_xTm, b_pos, b_gst = Buf("xTm"), Buf("posm"), Buf("gst")
        K.dma("sp", posi, pos_loc, b_pos, W=[b_pos])
        K.op("dve", lambda: nc.vector.tensor_copy(out=posf, in_=posi), R=[b_pos], W=[b_pos])

        for blk in range(cfg.NBLK):
            tok0 = blk * T
            o = AB0
            xin2 = []
            for i in range(2):
                a_, o = carve(arena, o, [128, D], F32)
                xin2.append(a_)
            b_xin = [Buf("xinB%d" % i) for i in range(2)]
            for c in range(NCH):
                xi_, bx = xin2[c % 2], b_xin[c % 2]
                K.dma("sp", xi_, x_loc[tok0 + c * C:tok0 + (c + 1) * C, :], bx, W=[bx])

                def ev_x(bap, bb, t0, n, c=c):
                    K.op("act", lambda: nc.scalar.copy(out=xT[:, t0:t0 + n, c * C:(c + 1) * C],
                                                      in_=bap.rearrange("p (a b) -> p a b", a=4)), R=[bb], W=[b_xTm])
                transpose_to(xi_, bx, KT, ev_x)
            K.barrier()
            chk(2)
            o = AB0
            zTa, o = carve(arena, o, [128, KT, T], BF16)
            b_zTa = Buf("zTa")
            trig = []
            b_trig = []
            for c in range(NCH):
                a1, o = carve(arena, o, [128, 256], F32)
                a2, o = carve(arena, o, [128, 256], F32)
                a3, o = carve(arena, o, [128, 256], F32)
                a4, o = carve(arena, o, [128, 256], F32)
                trig.append((a1, a2, a3, a4))
                b_trig.append(Buf("trig%d" % c))
            rot, o = carve(arena, o, [128, 4, 128], F32)
            rt1, o = carve(arena, o, [128, 4, 64], F32)
            rt2, o = carve(arena, o, [128, 4, 64], F32)
            qT, o = carve(arena, o, [128, 2, T], BF16)
            qxT, o = carve(arena, o, [128, 2, T], BF16)
            kT, o = carve(arena, o, [128, 2, T], BF16)
            kz, o = carve(arena, o, [128, NCH, 2, 128], BF16)
            vb, o = carve(arena, o, [128, NCH, 512], BF16)
            sg, o = carve(arena, o, [128, NCH, 512], F32)
            pm, o = carve(arena, o, [128, 2, 128], BF16)
            zp, o = carve(arena, o, [128, 512], F32)
            gmv, o = carve(arena, o, [128, 2, 2], F32)
            grs, o = carve(arena, o, [128, 2], F32)
            gtm, o = carve(arena, o, [128, 2], F32)
            gnb, o = carve(arena, o, [128, 2], F32)
            assert o <= ARENA_B, o
            b_rot, b_rt, b_qT, b_qxT, b_kT = Buf("rot"), Buf("rt"), Buf("qT"), Buf("qxT"), Buf("kT")
            b_kz = [Buf("kz%d" % c) for c in range(NCH)]
            b_vb = [Buf("vb%d" % c) for c in range(NCH)]
            b_sg = [Buf("sg%d" % c) for c in range(NCH)]
            b_pm, b_zp, b_gn = Buf("pm"), Buf("zp"), Buf("gn")
            for c in range(NCH):
                ang, tmp, sn, cs = trig[c]
                gi = (tok0 // C) + c
                trig_chunk(posf[:, gi:gi + 1], ang, tmp, sn, cs, b_pos, b_trig[c])
            for hp in range(4):
                h0 = hp * 2
                si_, rb, slot = acquire_b()
                wv = v_k512(slot)
                for c in range(NCH):
                    ct = slice(c * C, (c + 1) * C)
                    bb, bap = pb()
                    chk(200)
                    mm_group(bap, bb, [(xT[:, kt, ct], wv[:, kt, :]) for kt in range(KT)], R=[b_xTm, rb])
                    chk(201)
                    ang, tmp, sn, cs = trig[c]
                    rotary(bap, bb, cs, sn, b_trig[c], rot, b_rot, rt1, rt2, b_rt)
                    chk(202)
                    K.op("dve", lambda c=c: nc.vector.tensor_tensor(out=kz[:, c, :, :], in0=rot[:, 2:4, :], in1=ZSR[:, h0:h0 + 2, :],
                                                                    op=OP.mult), R=[b_rot, b_cp], W=[b_kz[c]])
                    chk(203)
                    b2, bap2 = pb()
                    for j in range(4):
                        K.op("pe", lambda j=j, bap2=bap2: nc.tensor.transpose(out=bap2[:, j * 128:(j + 1) * 128], in_=rot[:, j, :],
                                                                              identity=ident), R=[b_rot, b_cp], W=[b2], inc=(j == 3))
                    chk(204)
                    K.op("act", lambda ct=ct, bap2=bap2: nc.scalar.copy(out=qT[:, :, ct], in_=bap2[:, 0:256].rearrange("p (a b) -> p a b", a=2)),
                         R=[b2], W=[b_qT])
                    chk(205)
                    K.op("dve", lambda ct=ct, bap2=bap2: nc.vector.tensor_tensor(out=qxT[:, :, ct], in0=bap2[:, 0:256].rearrange("p (a b) -> p a b", a=2),
                                                                                 in1=XI[:, h0:h0 + 2, :], op=OP.mult), R=[b2, b_cp], W=[b_qxT])
                    K.op("act", lambda ct=ct, bap2=bap2: nc.scalar.copy(out=kT[:, :, ct], in_=bap2[:, 256:512].rearrange("p (a b) -> p a b", a=2)),
                         R=[b2], W=[b_kT])
                chk(21)
                si_, rb, slot = acquire_b()
                wv = v_k512(slot)
                for c in range(NCH):
                    ct = slice(c * C, (c + 1) * C)
                    bb, bap = pb()
                    mm_group(bap, bb, [(xT[:, kt, ct], wv[:, kt, :]) for kt in range(KT)], R=[b_xTm, rb])
                    K.op("act", lambda c=c, bap=bap: nc.scalar.copy(out=vb[:, c, :], in_=bap), R=[bb], W=[b_vb[c]])
                chk(22)
                si_, rb, slot = acquire_b()
                wv = v_k512(slot)
                for c in range(NCH):
                    ct = slice(c * C, (c + 1) * C)
                    bb, bap = pb()
                    mm_group(bap, bb, [(xT[:, kt, ct], wv[:, kt, :]) for kt in range(KT)], R=[b_xTm, rb])
                    K.op("act", lambda c=c, bap=bap: nc.scalar.activation(out=sg[:, c, :], in_=bap, func=AF.Silu), R=[bb], W=[b_sg[c]])
                chk(23)
                for c in range(NCH):
                    ct = slice(c * C, (c + 1) * C)
                    bs_, saps = pb()
                    for hh in range(2):
                        K.op("pe", lambda hh=hh, saps=saps, ct=ct: nc.tensor.matmul(saps[:, hh * 128:(hh + 1) * 128], lhsT=kT[:, hh, ct],
                                                                                   rhs=qT[:, hh, ct], start=True, stop=True),
                             R=[b_kT, b_qT], W=[bs_], inc=(hh == 1))
                    K.op("dve", lambda saps=saps: nc.vector.tensor_tensor(out=pm, in0=saps[:, 0:256].rearrange("p (a b) -> p a b", a=2),
                                                                          in1=maskT[:, h0:h0 + 2, :], op=OP.mult), R=[bs_, b_cp], W=[b_pm])
                    bo_, oap = pb()
                    for hh in range(2):
                        h = h0 + hh
                        K.op("pe", lambda hh=hh, oap=oap, c=c: nc.tensor.matmul(oap[:, hh * DV:(hh + 1) * DV], lhsT=pm[:, hh, :],
                                                                                rhs=vb[:, c, hh * DV:(hh + 1) * DV], start=True, stop=False),
                             R=[b_pm, b_vb[c]], W=[bo_], inc=False)
                        K.op("pe", lambda hh=hh, h=h, oap=oap, ct=ct: nc.tensor.matmul(oap[:, hh * DV:(hh + 1) * DV], lhsT=qxT[:, hh, ct],
                                                                                       rhs=Sbf[:, h * DV:(h + 1) * DV], start=False, stop=True),
                             R=[b_qxT, b_Sbf[h]], W=[bo_], inc=(hh == 1))
                    chk(24)
                    bt_, tap = pb()
                    for hh in range(2):
                        K.op("pe", lambda hh=hh, tap=tap, c=c: nc.tensor.matmul(tap[:, hh * DV:(hh + 1) * DV], lhsT=kz[:, c, hh, :],
                                                                                rhs=vb[:, c, hh * DV:(hh + 1) * DV], start=True, stop=True),
                             R=[b_kz[c], b_vb[c]], W=[bt_], inc=(hh == 1))
                    for hh in range(2):
                        h = h0 + hh
                        K.op("dve", lambda h=h, hh=hh, tap=tap: nc.vector.scalar_tensor_tensor(
                            out=S32[:, h * DV:(h + 1) * DV], in0=S32[:, h * DV:(h + 1) * DV], scalar=GC[:, h:h + 1],
                            in1=tap[:, hh * DV:(hh + 1) * DV], op0=OP.mult, op1=OP.add), R=[bt_, b_cp], W=[b_S32[h]])
                        K.op("act", lambda h=h: nc.scalar.copy(out=Sbf[:, h * DV:(h + 1) * DV], in_=S32[:, h * DV:(h + 1) * DV]),
                             R=[b_S32[h]], W=[b_Sbf[h]])
                    chk(25)
                    for hh in range(2):
                        K.op("dve", lambda hh=hh, oap=oap: nc.vector.bn_stats(out=gst[:, hh, :], in_=oap[:, hh * DV:(hh + 1) * DV]),
                             R=[bo_], W=[b_gst])
                        K.op("dve", lambda hh=hh: nc.vector.bn_aggr(out=gmv[:, hh, :], in_=gst[:, hh, :]), R=[b_gst], W=[b_gn])
                    rstd_from_var(gmv[:, :, 1], grs, gtm, [b_gn, b_cc], [b_gn])
                    for hh in range(2):
                        K.op("dve", lambda hh=hh, oap=oap: nc.vector.tensor_scalar(out=zp[:, hh * DV:(hh + 1) * DV], in0=oap[:, hh * DV:(hh + 1) * DV],
                                                                                   scalar1=gmv[:, hh, 0:1], scalar2=grs[:, hh:hh + 1],
                                                                                   op0=OP.subtract, op1=OP.mult), R=[bo_, b_gn], W=[b_zp])
                    K.op("dve", lambda c=c: nc.vector.tensor_tensor(out=zp, in0=zp, in1=sg[:, c, :], op=OP.mult), R=[b_sg[c]], W=[b_zp])

                    chk(26)

                    def ev_z(bap, bb, t0, n, ct=ct):
                        for j in range(n):
                            kt = hp * 4 + t0 + j
                            K.op("act", lambda j=j, kt=kt: nc.scalar.activation(out=zTa[:, kt, ct], in_=bap[:, j * 128:(j + 1) * 128],
                                                                                func=AF.Identity, scale=gng[:, kt:kt + 1]),
                                 R=[bb, b_cp], W=[b_zTa])
                    transpose_to(zp, b_zp, 4, ev_z)
            K.barrier()
            chk(3)
            o = AB0 + KT * T * 2
            gv, o = carve(arena, o, [128, NCH, D], F32)
            nb, o = carve(arena, o, [128, NCH, D], BF16)
            lmv, o = carve(arena, o, [128, NCH, 2], F32)
            lrs, o = carve(arena, o, [128, NCH], F32)
            ltm, o = carve(arena, o, [128, NCH], F32)
            assert o <= ARENA_B, o
            b_gv = [Buf("gv%d" % c) for c in range(NCH)]
            b_nb = [Buf("nb%d" % c) for c in range(NCH)]
            b_ln = Buf("ln")
            for j in range(4):
                si_, rb, slot = acquire_b()
                wv = v_k512(slot)
                for c in range(NCH):
                    ct = slice(c * C, (c + 1) * C)
                    bb, bap = pb()
                    mm_group(bap, bb, [(xT[:, kt, ct], wv[:, kt, :]) for kt in range(KT)], R=[b_xTm, rb])
                    K.op("act", lambda c=c, j=j, bap=bap: nc.scalar.activation(out=gv[:, c, j * 512:(j + 1) * 512], in_=bap, func=AF.Gelu),
                         R=[bb], W=[b_gv[c]])
            for c in range(NCH):
                layer_stats(gv[:, c, :], b_gv[c], lmv[:, c, :], gst, b_gst)
                K.op("dve", lambda c=c: nc.vector.tensor_copy(out=ltm[:, c:c + 1], in_=lmv[:, c, 1:2]), R=[b_gst], W=[b_ln])
            rstd_from_var(ltm, lrs, ltm, [b_ln, b_cc], [b_ln])
            for c in range(NCH):
                K.op("dve", lambda c=c: nc.vector.tensor_scalar(out=nb[:, c, :], in0=gv[:, c, :], scalar1=lmv[:, c, 0:1], scalar2=lrs[:, c:c + 1],
                                                                op0=OP.subtract, op1=OP.mult), R=[b_gv[c], b_ln, b_gst], W=[b_nb[c]])
            K.barrier()
            chk(4)
            o = AB0 + KT * T * 2 + NCH * D * 4
            o = AB0 + KT * T * 2
            ysT, o = carve(arena, o, [128, KT, T], BF16)
            guT, o = carve(arena, o, [128, 2, T], F32)
            stm, o = carve(arena, o, [128, 2, T], F32)
            assert o <= AB0 + KT * T * 2 + NCH * D * 4, o
            b_ysT = Buf("ysT")
            b_gu = [Buf("gu%d" % i) for i in range(2)]
            b_stm = [Buf("stm%d" % i) for i in range(2)]
            for j in range(4):
                si_, rb, slot = acquire_b()
                wv = v_k512(slot)
                for mt in range(4):
                    g = 4 * j + mt
                    bb, bap = pb()
                    mm_group(bap, bb, [(wv[:, kt, mt * 128:(mt + 1) * 128], xT[:, kt, :]) for kt in range(KT)], R=[b_xTm, rb])
                    i2 = g % 2
                    K.op("act", lambda i2=i2, bap=bap: nc.scalar.activation(out=guT[:, i2, :], in_=bap, func=AF.Gelu), R=[bb], W=[b_gu[i2]])
                    bm, map_ = pb()
                    for c in range(NCH):
                        K.op("pe", lambda c=c, g=g, map_=map_: nc.tensor.matmul(map_[:, c * C:(c + 1) * C], lhsT=nb[:, c, g * 128:(g + 1) * 128],
                                                                                rhs=WsT[:, g, :], start=True, stop=True),
                             R=[b_nb[c], b_WsT], W=[bm], inc=(c == NCH - 1))
                    for c in range(NCH):
                        K.op("dve", lambda c=c, g=g, i2=i2, map_=map_: nc.vector.scalar_tensor_tensor(
                            out=stm[:, i2, c * C:(c + 1) * C], in0=map_[:, c * C:(c + 1) * C], scalar=sgug[:, g:g + 1], in1=B2[:, g, :],
                            op0=OP.mult, op1=OP.add), R=[bm, b_B2, b_cp], W=[b_stm[i2]])
                    K.op("dve", lambda g=g, i2=i2: nc.vector.tensor_tensor(out=ysT[:, g, :], in0=stm[:, i2, :], in1=guT[:, i2, :], op=OP.mult),
                         R=[b_stm[i2], b_gu[i2]], W=[b_ysT])
            K.barrier()
            chk(5)
            o = AB0 + 2 * KT * T * 2
            mgT, o = carve(arena, o, [128, KT, T], BF16)
            sgA, o = carve(arena, o, [128, 4, T], F32)
            sgB, o = carve(arena, o, [128, 4, T], F32)
            mm_, o = carve(arena, o, [128, 4, T], F32)
            tt_, o = carve(arena, o, [128, T], F32)
            assert o <= ARENA_B, o
            b_mgT = Buf("mgT")
            b_sgA = [Buf("sgA%d" % i) for i in range(4)]
            b_sgB = [Buf("sgB%d" % i) for i in range(4)]
            b_mm = [Buf("mm%d" % i) for i in range(4)]
            b_tt = Buf("tt")
            for j in range(4):
                for br, sgt, bsg in ((0, sgA, b_sgA), (1, sgB, b_sgB)):
                    si_, rb, slot = acquire_b()
                    wv = v_k512(slot)
                    for mt in range(4):
                        bb, bap = pb()
                        mm_group(bap, bb, [(wv[:, kt, mt * 128:(mt + 1) * 128], xT[:, kt, :]) for kt in range(KT)], R=[b_xTm, rb])
                        K.op("act", lambda mt=mt, br=br, sgt=sgt, bap=bap: nc.scalar.activation(
                            out=sgt[:, mt, :], in_=bap, func=AF.Sigmoid, bias=bgate[:, br, 4 * j + mt:4 * j + mt + 1], scale=1.0),
                            R=[bb, b_cp], W=[bsg[mt]])
                si_, rb, slot = acquire_b()
                wv = v_k512(slot)
                for mt in range(4):
                    bb, bap = pb()
                    mm_group(bap, bb, [(wv[:, kt, mt * 128:(mt + 1) * 128], zTa[:, kt, :]) for kt in range(KT)], R=[b_zTa, rb])
                    K.op("dve", lambda mt=mt, bap=bap: nc.vector.tensor_tensor(out=mm_[:, mt, :], in0=bap, in1=sgA[:, mt, :], op=OP.mult),
                         R=[bb, b_sgA[mt]], W=[b_mm[mt]])
                si_, rb, slot = acquire_b()
                wv = v_k512(slot)
                for mt in range(4):
                    bb, bap = pb()
                    mm_group(bap, bb, [(wv[:, kt, mt * 128:(mt + 1) * 128], ysT[:, kt, :]) for kt in range(KT)], R=[b_ysT, rb])
                    K.op("dve", lambda mt=mt, bap=bap: nc.vector.tensor_tensor(out=tt_, in0=bap, in1=sgB[:, mt, :], op=OP.mult),
                         R=[bb, b_sgB[mt]], W=[b_tt])
                    K.op("dve", lambda mt=mt: nc.vector.tensor_tensor(out=mgT[:, 4 * j + mt, :], in0=tt_, in1=mm_[:, mt, :], op=OP.add),
                         R=[b_tt, b_mm[mt]], W=[b_mgT])
            K.barrier()
            chk(6)
            o = AB0
            x1 = []
            for c in range(NCH):
                a_, o = carve(arena, o, [128, D], F32)
                x1.append(a_)
            assert o <= AB0 + 2 * KT * T * 2
            o = AB0 + 3 * KT * T * 2
            g1, o = carve(arena, o, [128, D], F32)
            b1, o = carve(arena, o, [128, D], F32)
            x1Tf, o = carve(arena, o, [128, KT, 128], F32)
            x1Tb = []
            for i in range(2):
                a_, o = carve(arena, o, [128, KT, 128], BF16)
                x1Tb.append(a_)
            lg, o = carve(arena, o, [128, 40], F32)
            rsm, o = carve(arena, o, [128, 64], F32)
            assert o <= ARENA_B, o
            b_x1 = [Buf("x1_%d" % c) for c in range(NCH)]
            b_g1, b_b1, b_x1Tf = Buf("g1"), Buf("b1"), Buf("x1Tf")
            b_x1Tb = [Buf("x1Tb%d" % i) for i in range(2)]
            b_rt_ = Buf("route")
            K.dma("sp", g1, ln1g_d, b_g1, W=[b_g1])
            K.dma("sp", b1, ln1b_d, b_b1, W=[b_b1])
            for c in range(NCH):
                K.dma("sp", x1[c], x_loc[tok0 + c * C:tok0 + (c + 1) * C, :], b_x1[c], W=[b_x1[c]])
            for j in range(4):
                si_, rb, slot = acquire_b()
                wv = v_k512(slot)
                for c in range(NCH):
                    ct = slice(c * C, (c + 1) * C)
                    bb, bap = pb()
                    mm_group(bap, bb, [(mgT[:, kt, ct], wv[:, kt, :]) for kt in range(KT)], R=[b_mgT, rb])
                    K.op("dve", lambda c=c, j=j, bap=bap: nc.vector.scalar_tensor_tensor(
                        out=x1[c][:, j * 512:(j + 1) * 512], in0=x1[c][:, j * 512:(j + 1) * 512], scalar=DN_ALPHA, in1=bap,
                        op0=OP.mult, op1=OP.add), R=[bb], W=[b_x1[c]])
            for c in range(NCH):
                gi = (tok0 // C) + c
                ct = slice(c * C, (c + 1) * C)
                layer_stats(x1[c], b_x1[c], rsm[:, 0:2], gst, b_gst)
                rstd_from_var(rsm[:, 1:2], rsm[:, 2:3], rsm[:, 3:4], [b_gst, b_cc], [b_rt_])
                K.op("dve", lambda c=c: nc.vector.tensor_scalar(out=x1[c], in0=x1[c], scalar1=rsm[:, 0:1], scalar2=rsm[:, 2:3],
                                                                op0=OP.subtract, op1=OP.mult), R=[b_rt_, b_gst], W=[b_x1[c]])
                K.op("dve", lambda c=c: nc.vector.tensor_tensor(out=x1[c], in0=x1[c], in1=g1, op=OP.mult), R=[b_g1], W=[b_x1[c]])
                K.op("dve", lambda c=c: nc.vector.tensor_tensor(out=x1[c], in0=x1[c], in1=b1, op=OP.add), R=[b_b1], W=[b_x1[c]])
                K.dma("sp", X1[tok0 + c * C:tok0 + (c + 1) * C, :], x1[c], b_x1[c], R=[b_x1[c]], W=[b_X1])
                xb, bxb = x1Tb[c % 2], b_x1Tb[c % 2]

                def ev_x1(bap, bb, t0, n, xb=xb, bxb=bxb):
                    K.op("act", lambda: nc.scalar.copy(out=x1Tf[:, t0:t0 + n, :], in_=bap.rearrange("p (a b) -> p a b", a=4)), R=[bb], W=[b_x1Tf])
                    if not cfg.sparse:
                        K.op("dve", lambda: nc.vector.tensor_copy(out=xb[:, t0:t0 + n, :], in_=bap.rearrange("p (a b) -> p a b", a=4)), R=[bb], W=[bxb])
                transpose_to(x1[c], b_x1[c], KT, ev_x1)
                if not cfg.sparse:
                    K.dma("sp", X1T[:, :, tok0 + c * C:tok0 + (c + 1) * C], xb, bxb, R=[bxb], W=[b_X1T])
                bb, bap = pb()
                mm_group(bap[:, 0:36], bb, [(x1Tf[:, kt, :], wr[:, kt, :]) for kt in range(KT)], R=[b_x1Tf, b_cp])
                R_ = [b_rt_]
                K.op("dve", lambda bap=bap: nc.vector.tensor_tensor(out=lg[:, 0:36], in0=bap[:, 0:36], in1=brrep, op=OP.add),
                     R=[bb, b_cp], W=R_)
                gl = lg[:, 0:4]
                el = lg[:, 4:36].rearrange("p (g e) -> p g e", g=4)
                gmax, negm, oneh, sume, esel, top8 = rsm[:, 8:9], rsm[:, 9:10], rsm[:, 10:14], rsm[:, 14:15], rsm[:, 16:24], rsm[:, 24:32]
                ex4, pgrp, dd, ee, p1, p2 = rsm[:, 32:36], rsm[:, 36:37], rsm[:, 37:38], rsm[:, 38:39], rsm[:, 39:40], rsm[:, 40:41]
                m1, m2, w8 = rsm[:, 41:49], rsm[:, 49:57], rsm[:, 56:64]
                K.op("dve", lambda: nc.vector.tensor_reduce(out=gmax, in_=gl, axis=mybir.AxisListType.X, op=OP.max), R=R_, W=R_)
                K.op("dve", lambda: nc.vector.tensor_scalar(out=oneh, in0=gl, scalar1=gmax, scalar2=None, op0=OP.is_equal), R=R_, W=R_)
                K.op("dve", lambda: nc.vector.tensor_scalar(out=negm, in0=gmax, scalar1=-1.0, scalar2=None, op0=OP.mult), R=R_, W=R_)
                K.op("act", lambda: nc.scalar.activation(out=ex4, in_=gl, func=AF.Exp, bias=negm, scale=1.0), R=R_, W=R_)
                K.op("dve", lambda: nc.vector.tensor_reduce(out=sume, in_=ex4, axis=mybir.AxisListType.X, op=OP.add), R=R_, W=R_)
                K.op("dve", lambda: nc.vector.reciprocal(out=pgrp, in_=sume), R=R_, W=R_)
                K.op("dve", lambda: nc.vector.tensor_scalar(out=esel, in0=el[:, 0, :], scalar1=oneh[:, 0:1], scalar2=None, op0=OP.mult), R=R_, W=R_)
                for g in range(1, 4):
                    K.op("dve", lambda g=g: nc.vector.scalar_tensor_tensor(out=esel, in0=el[:, g, :], scalar=oneh[:, g:g + 1], in1=esel,
                                                                           op0=OP.mult, op1=OP.add), R=R_, W=R_)
                K.op("dve", lambda: nc.vector.max(out=top8, in_=esel), R=R_, W=R_)
                K.op("dve", lambda: nc.vector.tensor_scalar(out=m1, in0=esel, scalar1=top8[:, 0:1], scalar2=None, op0=OP.is_equal), R=R_, W=R_)
                K.op("dve", lambda: nc.vector.tensor_tensor(out=dd, in0=top8[:, 1:2], in1=top8[:, 0:1], op=OP.subtract), R=R_, W=R_)
                K.op("act", lambda: nc.scalar.activation(out=ee, in_=dd, func=AF.Exp), R=R_, W=R_)
                K.op("dve", lambda: nc.vector.tensor_scalar(out=p2, in0=ee, scalar1=1.0, scalar2=None, op0=OP.add), R=R_, W=R_)
                K.op("dve", lambda: nc.vector.reciprocal(out=p1, in_=p2), R=R_, W=R_)
                K.op("dve", lambda: nc.vector.tensor_tensor(out=p2, in0=ee, in1=p1, op=OP.mult), R=R_, W=R_)
                K.op("dve", lambda: nc.vector.tensor_tensor(out=p1, in0=p1, in1=pgrp, op=OP.mult), R=R_, W=R_)
                K.op("dve", lambda: nc.vector.tensor_tensor(out=p2, in0=p2, in1=pgrp, op=OP.mult), R=R_, W=R_)
                K.op("dve", lambda: nc.vector.tensor_scalar(out=w8, in0=esel, scalar1=top8[:, 1:2], scalar2=p2, op0=OP.is_equal, op1=OP.mult),
                     R=R_, W=R_)
                K.op("dve", lambda: nc.vector.scalar_tensor_tensor(out=w8, in0=m1, scalar=p1, in1=w8, op0=OP.mult, op1=OP.add), R=R_, W=R_)
                for g in range(4):
                    K.op("dve", lambda g=g, gi=gi: nc.vector.tensor_scalar(out=comb[:, gi, g * 8:(g + 1) * 8], in0=w8, scalar1=oneh[:, g:g + 1],
                                                                           scalar2=None, op0=OP.mult), R=R_, W=[b_comb[gi]])
            K.barrier()

        chk(7)
        if cfg.sparse:
            TS = 256
            NST = (2 * TPC) // TS + cfg.NE
            NTILE = NST * (TS // C)
            NSLOT = NST * TS
            assert NSLOT <= 12288 and NST <= 64
            BIG = 65536.0
            OOB = 4096
            sl["rel"].add(sl["next"] - 1)
            kb = keep
            o = 0
            NCE = NCHUNK * cfg.NE if cfg.NE == NE else NCHUNK * NE
            mask, o = carve(kb, o, [128, NCHUNK, NE], F32)
            POS, o = carve(kb, o, [128, NCHUNK, NE], F32)
            CNT, o = carve(kb, o, [128, NCHUNK, NE], F32)
            OFFC, o = carve(kb, o, [128, NCHUNK, NE], F32)
            TMPB, o = carve(kb, o, [128, NCHUNK, NE], F32)
            Umat, o = carve(kb, o, [128, 128], F32)
            sm, o = carve(kb, o, [128, 8, NE], F32)
            pA1, o = carve(kb, o, [128, NCHUNK], F32)
            pB1, o = carve(kb, o, [128, NCHUNK], F32)
            dupf, o = carve(kb, o, [128, NCHUNK], F32)
            pAi, o = carve(kb, o, [128, NCHUNK], I32)
            pBi, o = carve(kb, o, [128, NCHUNK], I32)
            REC, o = carve(kb, o, [128, NCHUNK, 2, 2], F32)
            t32, o = carve(kb, o, [128, NE], F32)
            eacc, o = carve(kb, o, [128, 64], F32)
            elist, o = carve(kb, o, [128, 64], I32)
            linit, o = carve(kb, o, [128, NSLOT // 128, 2], I32)
            cst, o = carve(kb, o, [128, 4, 6], F32)
            cmv, o = carve(kb, o, [128, 8], F32)
            assert o <= 24576, o
            tgrid, tokgrid = cpv("tgrid"), cpv("tokgrid")
            RECi = REC.bitcast(I32) if False else None
            rg_w = nc.gpsimd.to_reg(NE * 128 - 1)
            rg_x = nc.gpsimd.to_reg(TPC - 1)
            rg_s = nc.gpsimd.to_reg(NSLOT - 1)
            b_bk = Buf("bk")
            b_el, b_LIST, b_YS = Buf("elist"), Buf("LIST"), Buf("YS")
            Rb = [b_bk]
            maskf = mask.rearrange("p a b -> p (a b)")
            combf = comb.rearrange("p a b -> p (a b)")
            K.op("dve", lambda: nc.vector.tensor_single_scalar(out=maskf, in_=combf, scalar=0.0, op=OP.is_gt), R=b_comb, W=Rb)
            K.op("dve", lambda: nc.vector.tensor_tensor(out=Umat, in0=trilT, in1=ident, op=OP.subtract), R=[b_cp], W=Rb)
            bb, bap = pb()
            K.op("pe", lambda: nc.tensor.matmul(bap[:, 0:NCHUNK * NE], lhsT=ones32, rhs=maskf, start=True, stop=True), R=[b_ones, b_bk], W=[bb])
            K.op("dve", lambda: nc.vector.tensor_copy(out=CNT.rearrange("p a b -> p (a b)"), in_=bap[:, 0:NCHUNK * NE]), R=[bb], W=Rb)
            bb2, bap2 = pb()
            K.op("pe", lambda: nc.tensor.matmul(bap2[:, 0:NCHUNK * NE], lhsT=Umat, rhs=maskf, start=True, stop=True), R=[b_bk], W=[bb2])
            K.op("dve", lambda: nc.vector.tensor_copy(out=POS.rearrange("p a b -> p (a b)"), in_=bap2[:, 0:NCHUNK * NE]), R=[bb2], W=Rb)
            K.op("dve", lambda: nc.vector.memset(OFFC[:, 0, :], 0.0), W=Rb)
            for ch in range(1, NCHUNK):
                K.op("dve", lambda ch=ch: nc.vector.tensor_tensor(out=OFFC[:, ch, :], in0=OFFC[:, ch - 1, :], in1=CNT[:, ch - 1, :], op=OP.add), R=Rb, W=Rb)
            TOT, NT, END, BASEv, T1, T2 = (sm[:, i, :] for i in range(6))
            K.op("dve", lambda: nc.vector.tensor_tensor(out=TOT, in0=OFFC[:, NCHUNK - 1, :], in1=CNT[:, NCHUNK - 1, :], op=OP.add), R=Rb, W=Rb)
            K.op("dve", lambda: nc.vector.tensor_scalar(out=T1, in0=TOT, scalar1=1.0 / TS, scalar2=((TS - 1.0) / TS - 0.5 + 0.5 / TS),
                                                        op0=OP.mult, op1=OP.add), R=Rb, W=Rb)
            K.op("dve", lambda: nc.vector.tensor_scalar(out=T2, in0=T1, scalar1=MAGIC, scalar2=None, op0=OP.add), R=Rb, W=Rb)
            K.op("dve", lambda: nc.vector.tensor_scalar(out=NT, in0=T2, scalar1=-MAGIC, scalar2=None, op0=OP.add), R=Rb, W=Rb)
            K.op("dve", lambda: nc.vector.tensor_copy(out=END[:, 0:1], in_=NT[:, 0:1]), R=Rb, W=Rb)
            for e in range(1, NE):
                K.op("dve", lambda e=e: nc.vector.tensor_tensor(out=END[:, e:e + 1], in0=END[:, e - 1:e], in1=NT[:, e:e + 1], op=OP.add), R=Rb, W=Rb)
            K.op("dve", lambda: nc.vector.tensor_tensor(out=BASEv, in0=END, in1=NT, op=OP.subtract), R=Rb, W=Rb)
            K.op("dve", lambda: nc.vector.tensor_scalar(out=BASEv, in0=BASEv, scalar1=float(TS), scalar2=None, op0=OP.mult), R=Rb, W=Rb)
            for ch in range(NCHUNK):
                K.op("dve", lambda ch=ch: nc.vector.tensor_tensor(out=OFFC[:, ch, :], in0=OFFC[:, ch, :], in1=BASEv, op=OP.add), R=Rb, W=Rb)
            POSf, OFFf, TMPf = (a.rearrange("p a b -> p (a b)") for a in (POS, OFFC, TMPB))
            K.op("dve", lambda: nc.vector.tensor_tensor(out=POSf, in0=POSf, in1=OFFf, op=OP.add), R=Rb, W=Rb)
            K.op("dve", lambda: nc.vector.scalar_tensor_tensor(out=POSf, in0=POSf, scalar=1.0, in1=maskf, op0=OP.add, op1=OP.mult), R=Rb, W=Rb)
            K.op("dve", lambda: nc.vector.tensor_scalar(out=TMPf, in0=maskf, scalar1=-BIG, scalar2=BIG, op0=OP.mult, op1=OP.add), R=Rb, W=Rb)
            K.op("dve", lambda: nc.vector.tensor_tensor(out=TMPf, in0=TMPf, in1=POSf, op=OP.add), R=Rb, W=Rb)
            K.op("dve", lambda: nc.vector.tensor_reduce(out=pA1, in_=POS, axis=mybir.AxisListType.X, op=OP.max), R=Rb, W=Rb)
            K.op("dve", lambda: nc.vector.tensor_reduce(out=pB1, in_=TMPB, axis=mybir.AxisListType.X, op=OP.min), R=Rb, W=Rb)
            K.op("dve", lambda: nc.vector.tensor_tensor(out=dupf, in0=pA1, in1=pB1, op=OP.not_equal), R=Rb, W=Rb)
            RECi = REC.bitcast(I32)
            for ch in range(NCHUNK):
                for wi, pp in ((0, pA1), (1, pB1)):
                    K.op("dve", lambda ch=ch, pp=pp: nc.vector.scalar_tensor_tensor(out=t32, in0=POS[:, ch, :], scalar=pp[:, ch:ch + 1], in1=comb[:, ch, :],
                                                                                    op0=OP.is_equal, op1=OP.mult), R=Rb + [b_comb[ch]], W=Rb)
                    K.op("dve", lambda ch=ch, wi=wi: nc.vector.tensor_reduce(out=REC[:, ch, wi, 1:2], in_=t32, axis=mybir.AxisListType.X, op=OP.add), R=Rb, W=Rb)
                    K.op("dve", lambda ch=ch, wi=wi: nc.vector.tensor_copy(out=RECi[:, ch, wi, 0:1], in_=tokgrid[:, ch:ch + 1]), R=Rb + [b_cp], W=Rb)
            K.op("dve", lambda: nc.vector.tensor_scalar(out=pA1, in0=pA1, scalar1=-1.0, scalar2=None, op0=OP.add), R=Rb, W=Rb)
            K.op("dve", lambda: nc.vector.tensor_scalar(out=pB1, in0=pB1, scalar1=-1.0, scalar2=None, op0=OP.add), R=Rb, W=Rb)
            K.op("dve", lambda: nc.vector.tensor_copy(out=pAi, in_=pA1), R=Rb, W=Rb)
            K.op("dve", lambda: nc.vector.tensor_copy(out=pBi, in_=pB1), R=Rb, W=Rb)
            K.op("dve", lambda: nc.vector.memset(eacc, 0.0), W=Rb)
            for e in range(NE):
                K.op("dve", lambda e=e: nc.vector.scalar_tensor_tensor(out=eacc, in0=tgrid, scalar=END[:, e:e + 1], in1=eacc, op0=OP.is_ge, op1=OP.add),
                     R=Rb + [b_cp], W=Rb)
            K.op("dve", lambda: nc.vector.tensor_scalar(out=eacc, in0=eacc, scalar1=float(NE - 1), scalar2=None, op0=OP.min), R=Rb, W=Rb)
            eneq, _o = carve(kb, o, [128, 64], F32)
            assert _o <= 24576
            K.op("dve", lambda: nc.vector.memset(eneq, 1.0), W=Rb)
            K.op("dve", lambda: nc.vector.tensor_tensor(out=eneq[:, 2:64], in0=eacc[:, 2:64], in1=eacc[:, 0:62], op=OP.not_equal), R=Rb, W=Rb)
            K.op("dve", lambda: nc.vector.tensor_scalar(out=eacc, in0=eacc, scalar1=128.0, scalar2=tokgrid[:, 0:1], op0=OP.mult, op1=OP.add), R=Rb + [b_cp], W=Rb)
            K.op("dve", lambda: nc.vector.scalar_tensor_tensor(out=eacc, in0=eacc, scalar=-8192.0, in1=eneq, op0=OP.add, op1=OP.mult), R=Rb, W=Rb)
            K.op("dve", lambda: nc.vector.tensor_scalar(out=eacc, in0=eacc, scalar1=8192.0, scalar2=None, op0=OP.add), R=Rb, W=Rb)
            K.op("dve", lambda: nc.vector.tensor_copy(out=elist, in_=eacc), R=Rb, W=[b_el])
            eidx = elist
            chk(70)
            K.op("dve", lambda: nc.vector.memset(linit[:, :, 0:1], OOB), W=Rb)
            K.op("dve", lambda: nc.vector.memset(linit[:, :, 1:2], 0), W=Rb)
            K.dma("sp", LIST[0:NSLOT, :].rearrange("(p a) b -> p a b", p=128), linit, b_bk, R=Rb, W=[b_LIST])
            K._wait("pool", [b_LIST.w] + K._deps(Rb, []))
            b_sc = Buf("scat")
            for ch in range(NCHUNK):
                for wi, pi_ in ((0, pAi), (1, pBi)):
                    K.dma_custom("pool", lambda ch=ch, wi=wi, pi_=pi_: nc.gpsimd.indirect_dma_start(
                        out=LIST[:, :], out_offset=bass.IndirectOffsetOnAxis(ap=pi_[:, ch:ch + 1], axis=0),
                        in_=RECi[:, ch, wi, :], in_offset=None, bounds_check=rg_s, oob_is_err=False), b_sc, R=Rb, W=[])
            b_LIST.w = (b_sc.sem["sw"][0], 16 * b_sc.sem["sw"][1])

            chk(71)
            big = sb[:, CP_B + KEEP_B:SB_TOTAL]
            BIGB = SB_TOTAL - CP_B - KEEP_B
            ringc = [big[:, i * 16384:(i + 1) * 16384].bitcast(BF16) for i in range(6)]
            ringc_b = [Buf("ringc%d" % i) for i in range(6)]
            o = 6 * 16384
            xg, xgT, hd, hdT, yrow, idxt = [], [], [], [], [], []
            for i in range(2):
                a_, o = carve(big, o, [128, D], F32); xg.append(a_)
                a_, o = carve(big, o, [128, KT, 128], BF16); xgT.append(a_)
                a_, o = carve(big, o, [128, 512], F32); hd.append(a_)
                a_, o = carve(big, o, [128, 4, 128], BF16); hdT.append(a_)
                a_, o = carve(big, o, [128, D], F32); yrow.append(a_)
                a_, o = carve(big, o, [128, 2], I32); idxt.append(a_)
            ssl, o = carve(big, o, [128, 512], F32)
            lnrep, o = carve(big, o, [128, 2, D], F32)
            assert o <= BIGB, (o, BIGB)
            b_xg = [Buf("xg%d" % i) for i in range(2)]
            b_xgT = [Buf("xgT%d" % i) for i in range(2)]
            b_hd = [Buf("hd%d" % i) for i in range(2)]
            b_hdT = [Buf("hdT%d" % i) for i in range(2)]
            b_yrow = [Buf("yrow%d" % i) for i in range(2)]
            b_idx = [Buf("idx%d" % i) for i in range(2)]
            b_ssl, b_ln2 = Buf("sslS"), Buf("ln2S")
            for i in range(2):
                K.op("dve", lambda i=i: nc.vector.memset(xg[i], 0.0), W=[b_xg[i]])
            K.dma("sp", lnrep[:, 0, :], ln2g_d, b_ln2, W=[b_ln2])
            K.dma("sp", lnrep[:, 1, :], ln2b_d, b_ln2, W=[b_ln2])


            def w_loads(st_):
                wp = st_ % 2
                for k, wd in enumerate((w1r, w3r, w2r)):
                    sidx = wp * 3 + k
                    dst = ringc[sidx]
                    K.dma_custom("pool", lambda wd=wd, dst=dst: nc.gpsimd.indirect_dma_start(
                        out=dst, out_offset=None, in_=wd[:, :], in_offset=bass.IndirectOffsetOnAxis(ap=eidx[:, st_:st_ + 1], axis=0),
                        bounds_check=rg_w, oob_is_err=False), ringc_b[sidx], R=[b_el], W=[ringc_b[sidx]])

            def tile_loads(t):
                par = t % 2
                K.dma("sp", idxt[par], LIST[t * C:(t + 1) * C, :], b_idx[par], R=[b_LIST], W=[b_idx[par]])
                K.dma_custom("pool", lambda: nc.gpsimd.indirect_dma_start(
                    out=xg[par], out_offset=None, in_=X1[:, :], in_offset=bass.IndirectOffsetOnAxis(ap=idxt[par][:, 0:1], axis=0),
                    bounds_check=rg_x, oob_is_err=False), b_xg[par], R=[b_idx[par], b_X1], W=[b_xg[par]])

            def tile_compute(t):
                par = t % 2
                wpar = (t // (TS // C)) % 2
                wv1 = ringc[wpar * 3 + 0].rearrange("p (k c) -> p k c", k=KT)
                wv3 = ringc[wpar * 3 + 1].rearrange("p (k c) -> p k c", k=KT)
                wv2 = ringc[wpar * 3 + 2].rearrange("p (k c) -> p k c", k=4)
                rb1, rb3, rb2 = ringc_b[wpar * 3], ringc_b[wpar * 3 + 1], ringc_b[wpar * 3 + 2]
                cw = idxt[par].bitcast(F32)[:, 1:2]

                def ev_xg(bap, bb, t0, n):
                    eng = "act" if (t0 // 4) % 2 == 0 else "dve"
                    if eng == "act":
                        K.op("act", lambda: nc.scalar.copy(out=xgT[par][:, t0:t0 + n, :], in_=bap.rearrange("p (a b) -> p a b", a=4)), R=[bb], W=[b_xgT[par]])
                    else:
                        K.op("dve", lambda: nc.vector.tensor_copy(out=xgT[par][:, t0:t0 + n, :], in_=bap.rearrange("p (a b) -> p a b", a=4)), R=[bb], W=[b_xgT[par]])
                transpose_to(xg[par], b_xg[par], KT, ev_xg)
                b1_, a1 = pb()
                mm_group(a1, b1_, [(xgT[par][:, kt, :], wv1[:, kt, :]) for kt in range(KT)], R=[b_xgT[par], rb1])
                b3_, a3 = pb()
                mm_group(a3, b3_, [(xgT[par][:, kt, :], wv3[:, kt, :]) for kt in range(KT)], R=[b_xgT[par], rb3])
                K.op("act", lambda: nc.scalar.activation(out=ssl, in_=a1, func=AF.Silu), R=[b1_], W=[b_ssl])
                K.op("dve", lambda: nc.vector.scalar_tensor_tensor(out=hd[par], in0=a3, scalar=cw, in1=ssl, op0=OP.mult, op1=OP.mult),
                     R=[b3_, b_ssl, b_idx[par]], W=[b_hd[par]])

                def ev_hd(bap, bb, t0, n):
                    K.op("act", lambda: nc.scalar.copy(out=hdT[par], in_=bap.rearrange("p (a b) -> p a b", a=4)), R=[bb], W=[b_hdT[par]])
                transpose_to(hd[par], b_hd[par], 4, ev_hd)
                for j in range(4):
                    bb, bap = pb()
                    mm_group(bap, bb, [(hdT[par][:, ft, :], wv2[:, ft, j * 512:(j + 1) * 512]) for ft in range(4)], R=[b_hdT[par], rb2])
                    if j % 2 == 0:
                        K.op("act", lambda j=j, bap=bap: nc.scalar.copy(out=yrow[par][:, j * 512:(j + 1) * 512], in_=bap), R=[bb], W=[b_yrow[par]])
                    else:
                        K.op("dve", lambda j=j, bap=bap: nc.vector.tensor_copy(out=yrow[par][:, j * 512:(j + 1) * 512], in_=bap), R=[bb], W=[b_yrow[par]])
                K.dma("sp", YS[t * C:(t + 1) * C, :], yrow[par], b_yrow[par], R=[b_yrow[par]], W=[b_YS])

            SUB = TS // C
            w_loads(0)
            if NST > 1:
                w_loads(1)
            tile_loads(0)
            tile_loads(1)
            for t in range(NTILE):
                tile_compute(t)
                if t + 2 < NTILE:
                    tile_loads(t + 2)
                if t % SUB == SUB - 1 and (t // SUB) + 2 < NST:
                    w_loads(t // SUB + 2)
            chk(74)
            fin = []
            for i in range(2):
                fin.append((xg[i], yrow[i], b_xg[i], b_yrow[i]))
            x1c, o2 = carve(big, 0, [128, 2, D], F32)
            b_x1c = [ringc_b[0], ringc_b[1]]
            for ch in range(NCHUNK):
                i = ch % 2
                rA, rB, brA, brB = fin[i]
                K.dma_custom("pool", lambda rA=rA, ch=ch: nc.gpsimd.indirect_dma_start(
                    out=rA, out_offset=None, in_=YS[:, :], in_offset=bass.IndirectOffsetOnAxis(ap=pAi[:, ch:ch + 1], axis=0),
                    bounds_check=rg_s, oob_is_err=False), brA, R=[b_YS, b_bk], W=[brA])
                K.dma_custom("pool", lambda rB=rB, ch=ch: nc.gpsimd.indirect_dma_start(
                    out=rB, out_offset=None, in_=YS[:, :], in_offset=bass.IndirectOffsetOnAxis(ap=pBi[:, ch:ch + 1], axis=0),
                    bounds_check=rg_s, oob_is_err=False), brB, R=[b_YS, b_bk], W=[brB])
                xc = x1c[:, i, :]
                bxc = b_x1c[i]
                K.dma("sp", xc, X1[ch * C:(ch + 1) * C, :], bxc, R=[b_X1], W=[bxc])
                K.op("dve", lambda xc=xc, rA=rA: nc.vector.scalar_tensor_tensor(out=xc, in0=xc, scalar=DN_ALPHA, in1=rA, op0=OP.mult, op1=OP.add),
                     R=[brA], W=[bxc])
                K.op("dve", lambda xc=xc, rB=rB, ch=ch: nc.vector.scalar_tensor_tensor(out=xc, in0=rB, scalar=dupf[:, ch:ch + 1], in1=xc, op0=OP.mult, op1=OP.add),
                     R=[brB, b_bk], W=[bxc])
                layer_stats(xc, bxc, cmv[:, 0:2], cst, b_bk)
                K.op("act", lambda: nc.scalar.activation(out=cmv[:, 3:4], in_=cmv[:, 1:2], func=AF.Sqrt, bias=eps_col[:, 0:1], scale=1.0),
                     R=[b_bk, b_cc], W=[b_bk])
                K.op("dve", lambda: nc.vector.reciprocal(out=cmv[:, 2:3], in_=cmv[:, 3:4]), R=[b_bk], W=[b_bk])
                K.op("dve", lambda xc=xc: nc.vector.tensor_scalar(out=xc, in0=xc, scalar1=cmv[:, 0:1], scalar2=cmv[:, 2:3],
                                                                  op0=OP.subtract, op1=OP.mult), R=[b_bk], W=[bxc])
                K.op("dve", lambda xc=xc: nc.vector.tensor_tensor(out=xc, in0=xc, in1=lnrep[:, 0, :], op=OP.mult), R=[b_ln2], W=[bxc])
                K.op("dve", lambda xc=xc: nc.vector.tensor_tensor(out=xc, in0=xc, in1=lnrep[:, 1, :], op=OP.add), R=[b_ln2], W=[bxc])
                K.dma("sp", y_out[ch * C:(ch + 1) * C, :], xc, bxc, R=[bxc])
            K.barrier()
            return
        PT = cfg.PT
        NPC = PT // C
        NTL = PT // 512
        sl["rel"].add(sl["next"] - 1)
        sl["limit"] = len(slabs)
        r1 = sb[:, 0:CP_B + 24576]
        o = 0
        lnrep, o = carve(r1, o, [128, 2, D], F32)
        hdn, o = carve(r1, o, [128, 2, 4, 512], BF16)
        ssl, o = carve(r1, o, [128, 2, 512], F32)
        cst, o = carve(r1, o, [128, 4, 6], F32)
        cmv, o = carve(r1, o, [128, 8], F32)
        epsC, o = carve(r1, o, [128, 1], F32)
        assert o <= CP_B + 24576, o
        r2 = sb[:, CP_B + KEEP_B + 4 * 16384:SB_TOTAL]
        o = 0
        x1Tm, o = carve(r2, o, [128, KT, PT], BF16)
        yacc_t, o = carve(r2, o, [128, NPC, D], F32)
        assert o <= SB_TOTAL - (CP_B + KEEP_B + 4 * 16384), o
        b_x1Tm, b_ln2 = Buf("x1Tm"), Buf("ln2")
        b_hdn = [[Buf("hdn%d_%d" % (i, f)) for f in range(4)] for i in range(2)]
        b_ssl = [Buf("ssl%d" % i) for i in range(2)]
        b_ya = [Buf("ya%d" % c) for c in range(NPC)]
        b_cst = Buf("cst")
        b_ccC = Buf("ccC")
        K.op("dve", lambda: nc.vector.memset(epsC, LN_EPS), W=[b_ccC])
        K.dma("sp", lnrep[:, 0, :], ln2g_d, b_ln2, W=[b_ln2])
        K.dma("sp", lnrep[:, 1, :], ln2b_d, b_ln2, W=[b_ln2])
        for p_ in range(cfg.NPASS):
            t0p = p_ * PT
            K.dma("sp", x1Tm, X1T[:, :, t0p:t0p + PT], b_x1Tm, R=[b_X1T], W=[b_x1Tm])
            for c in range(NPC):
                K.dma("sp", yacc_t[:, c, :], X1[t0p + c * C:t0p + (c + 1) * C, :], b_ya[c], R=[b_X1], W=[b_ya[c]])
                K.op("act", lambda c=c: nc.scalar.mul(out=yacc_t[:, c, :], in_=yacc_t[:, c, :], mul=DN_ALPHA), W=[b_ya[c]])
            hcnt = 0
            for e in range(cfg.NE):
                i1, rb1, s1 = acquire()
                i3, rb3, s3 = acquire()
                wv1, wv3 = v_k512(s1), v_k512(s3)

                def do_h(tl, hb):
                    tt = slice(tl * 512, (tl + 1) * 512)
                    for ft in range(4):
                        b1_, a1 = pb()
                        mm_group(a1, b1_, [(wv1[:, kt, ft * 128:(ft + 1) * 128], x1Tm[:, kt, tt]) for kt in range(KT)], R=[b_x1Tm, rb1])
                        b3_, a3 = pb()
                        mm_group(a3, b3_, [(wv3[:, kt, ft * 128:(ft + 1) * 128], x1Tm[:, kt, tt]) for kt in range(KT)], R=[b_x1Tm, rb3])
                        i2 = ft % 2
                        K.op("act", lambda a1=a1, i2=i2: nc.scalar.activation(out=ssl[:, i2, :], in_=a1, func=AF.Silu), R=[b1_], W=[b_ssl[i2]])
                        K.op("dve", lambda a3=a3, i2=i2, ft=ft: nc.vector.tensor_tensor(out=hdn[:, hb, ft, :], in0=a3, in1=ssl[:, i2, :], op=OP.mult),
                             R=[b3_, b_ssl[i2]], W=[b_hdn[hb][ft]])

                def do_y(tl, hb, wv2, rb2):
                    for cc in range(4):
                        c = tl * 4 + cc
                        gi = t0p // C + c
                        for j in range(4):
                            bb, bap = pb()
                            mm_group(bap, bb, [(hdn[:, hb, ft, cc * C:(cc + 1) * C], wv2[:, ft, j * 512:(j + 1) * 512]) for ft in range(4)],
                                     R=[b_hdn[hb][0], b_hdn[hb][1], b_hdn[hb][2], b_hdn[hb][3], rb2])
                            K.op("dve", lambda c=c, j=j, gi=gi, bap=bap: nc.vector.scalar_tensor_tensor(
                                out=yacc_t[:, c, j * 512:(j + 1) * 512], in0=bap, scalar=comb[:, gi, e:e + 1], in1=yacc_t[:, c, j * 512:(j + 1) * 512],
                                op0=OP.mult, op1=OP.add), R=[bb, b_comb[gi]], W=[b_ya[c]])
                do_h(0, hcnt % 2)
                i2_ = None
                for tl in range(NTL):
                    if tl + 1 < NTL:
                        do_h(tl + 1, (hcnt + 1) % 2)
                    else:
                        release(i1)
                        release(i3)
                    if i2_ is None:
                        i2_, rb2, s2 = acquire()
                        wv2 = v_k4(s2)
                    do_y(tl, hcnt % 2, wv2, rb2)
                    hcnt += 1
                release(i2_)
            for c in range(NPC):
                ya = yacc_t[:, c, :]
                layer_stats(ya, b_ya[c], cmv[:, 0:2], cst, b_cst)
                K.op("act", lambda: nc.scalar.activation(out=cmv[:, 3:4], in_=cmv[:, 1:2], func=AF.Sqrt, bias=epsC[:, 0:1], scale=1.0),
                     R=[b_cst, b_ccC], W=[b_cst])
                K.op("dve", lambda: nc.vector.reciprocal(out=cmv[:, 2:3], in_=cmv[:, 3:4]), R=[b_cst], W=[b_cst])
                K.op("dve", lambda ya=ya: nc.vector.tensor_scalar(out=ya, in0=ya, scalar1=cmv[:, 0:1], scalar2=cmv[:, 2:3],
                                                                  op0=OP.subtract, op1=OP.mult), R=[b_cst], W=[b_ya[c]])
                K.op("dve", lambda ya=ya: nc.vector.tensor_tensor(out=ya, in0=ya, in1=lnrep[:, 0, :], op=OP.mult), R=[b_ln2], W=[b_ya[c]])
                K.op("dve", lambda ya=ya: nc.vector.tensor_tensor(out=ya, in0=ya, in1=lnrep[:, 1, :], op=OP.add), R=[b_ln2], W=[b_ya[c]])
                K.dma("sp", y_out[t0p + c * C:t0p + (c + 1) * C, :], ya, b_ya[c], R=[b_ya[c]])
            K.barrier()
        K.barrier()


def module_constants():
    lay, w = CPK, CPK_W
    cp = np.zeros((128, w), np.float32)

    def put(name, arr):
        o, ww = lay[name]
        cp[:, o:o + ww] = np.asarray(arr, np.float32).reshape(128, ww)
    put("ident", np.eye(128, dtype=np.float32))
    hh = np.arange(H, dtype=np.float32)
    log_gamma = np.log1p(-np.exp2(-5.0 - hh)).astype(np.float32)
    idx = np.arange(C, dtype=np.float32)
    rel = idx[:, None] - idx[None, :]
    dm = np.where(rel >= 0, np.exp(log_gamma[:, None, None] * np.maximum(rel, 0.0)), 0.0)
    scale = np.float32(DK ** -0.5)
    put("maskT", (dm.transpose(2, 0, 1) * scale))
    xi = np.exp(log_gamma[:, None] * (idx + 1.0))
    put("XI", np.broadcast_to(xi[None], (128, H, C)))
    zeta = np.exp(log_gamma[:, None] * (C - 1.0 - idx))
    put("ZSR", np.broadcast_to((zeta.T * scale)[:, :, None], (128, H, DK)))
    put("GC", np.broadcast_to(np.exp(log_gamma * C)[None], (128, H)))
    half = DK // 2
    freq = (np.float32(10000.0) ** (-np.arange(half, dtype=np.float32) / np.float32(half))).astype(np.float32)
    put("freq4", np.broadcast_to(np.tile(freq, 4)[None], (128, 256)))
    s_ = np.arange(128)
    put("trilT", (s_[:, None] <= s_[None, :]).astype(np.float32))
    put("tgrid", np.broadcast_to(np.arange(64, dtype=np.float32)[None], (128, 64)))
    put("tokgrid", (np.arange(16, dtype=np.float32)[None, :] * 128 + np.arange(128, dtype=np.float32)[:, None]))
    return cp


def prep_inputs(inp, cfg, n_cores, cores_per_row):
    cpk0 = module_constants()
    f = lambda a: np.ascontiguousarray(np.asarray(a, np.float32))
    x = f(inp["x"])
    pos = np.asarray(inp["positions"]).astype(np.int32)
    lay = CPK

    def put(cp, name, arr):
        o, ww = lay[name]
        cp[:, o:o + ww] = np.asarray(arr, np.float32).reshape(128, ww)
    cp = cpk0.copy()
    bg = f(inp["b_gate"])[0]
    put(cp, "bgate", bg.reshape(2, KT, 128).transpose(2, 0, 1))
    put(cp, "gng", f(inp["ret_gn_g"])[0].reshape(KT, 128).T)
    put(cp, "sgug", f(inp["sgu_ln_g"])[0].reshape(KT, 128).T)
    put(cp, "sgub", f(inp["sgu_ln_b"])[0].reshape(KT, 128).T)
    wrr = np.concatenate([f(inp["w_group"])[0], f(inp["w_er"])[0]], axis=1)
    put(cp, "wr", wrr.reshape(KT, 128, 36).transpose(1, 0, 2))
    brr = np.concatenate([f(inp["b_group"])[0], f(inp["b_er"])[0]])
    put(cp, "brrep", np.broadcast_to(brr[None], (128, 36)))
    wsT = f(inp["sgu_w"])[0].transpose(2, 0, 1).reshape(128, 16 * 128)
    bsrep = np.ascontiguousarray(np.broadcast_to(f(inp["sgu_b"])[0][None], (128, 16, 128))).reshape(128, 2048)
    rep = lambda v: np.ascontiguousarray(np.broadcast_to(f(v)[0][None], (128, D)))
    shared = {
        "cpk": cp, "wsT": np.ascontiguousarray(wsT), "bsrep": bsrep,
        "ln1g": rep(inp["ln1_g"]), "ln1b": rep(inp["ln1_b"]), "ln2g": rep(inp["ln2_g"]), "ln2b": rep(inp["ln2_b"]),
        "w_in": f(inp["w_in"])[0], "w_proj_ret": f(inp["w_proj_ret"])[0], "w_proj_sgu": f(inp["w_proj_sgu"])[0],
        "w_out": f(inp["w_out"])[0], "w1": f(inp["w1"])[0], "w3": f(inp["w3"])[0], "w2": f(inp["w2"])[0],
    }
    shared["w1r"] = np.ascontiguousarray(shared["w1"].reshape(NE, KT, 128, DE).transpose(0, 2, 1, 3)).reshape(NE * 128, 8192)
    shared["w3r"] = np.ascontiguousarray(shared["w3"].reshape(NE, KT, 128, DE).transpose(0, 2, 1, 3)).reshape(NE * 128, 8192)
    shared["w2r"] = np.ascontiguousarray(shared["w2"].reshape(NE, 4, 128, D).transpose(0, 2, 1, 3)).reshape(NE * 128, 8192)
    maps = []
    TPC, NPRE = cfg.TPC, cfg.NPRE
    for c in range(n_cores):
        b, seg = c // cores_per_row, c % cores_per_row
        t0 = seg * TPC
        m = dict(shared)
        m["x_loc"] = np.ascontiguousarray(x[b, t0:t0 + TPC])
        npre_tok = max(NPRE, 1) * C
        xp = np.zeros((npre_tok, D), np.float32)
        pp = np.zeros((npre_tok,), np.int32)
        if t0 > 0:
            xp[npre_tok - t0:] = x[b, :t0]
            pp[npre_tok - t0:] = pos[b, :t0]
        m["x_pre"] = xp
        m["pos_pre"] = np.ascontiguousarray(pp.reshape(-1, 128).T)
        m["pos_loc"] = np.ascontiguousarray(pos[b, t0:t0 + TPC].reshape(-1, 128).T)
        maps.append(m)
    return maps


def run(inp, cfg, n_cores, cores_per_row, trace=False):
    nc = build(cfg)
    maps = prep_inputs(inp, cfg, n_cores, cores_per_row)
    res = run_bass_kernel_spmd(nc, maps, core_ids=list(range(n_cores)), trace=trace)
    B = n_cores // cores_per_row
    out = np.stack([np.concatenate([np.asarray(res.results[b * cores_per_row + s]["y"]) for s in range(cores_per_row)], axis=0)
                    for b in range(B)], axis=0)
    return out.astype(np.float32), res


def kernel(**inputs):
    cfg = CFG(tpc=2048, npre=48, nch=4, pass_tok=1024)
    out, _ = run(inputs, cfg, 8, 4)
    return out
```

```python
import math
from contextlib import ExitStack

import numpy as np
import concourse.bass as bass
import concourse.mybir as mybir
from concourse.bass_utils import run_bass_kernel_spmd

F32 = mybir.dt.float32
BF16 = mybir.dt.bfloat16
I32 = mybir.dt.int32
U8 = mybir.dt.uint8
AF = mybir.ActivationFunctionType
OP = mybir.AluOpType

D = 2048
KT = 16
H = 8
DK = 128
DV = 256
C = 128
NE = 32
DE = 512
IN_W = 14336
LN_EPS = 1e-5
DN_ALPHA = 2.0 ** 0.25
OFF_Q, OFF_K, OFF_V, OFF_GR, OFF_U, OFF_VS, OFF_GA, OFF_GB = 0, 1024, 2048, 4096, 6144, 8192, 10240, 12288
MAGIC = 12582912.0
TWO_PI_HI = 6.28125
TWO_PI_LO = 2.0 * math.pi - 6.28125
PI_SAFE = 3.1415925
SAME_ENG_SYNC = True


class Buf:
    __slots__ = ("name", "w", "r", "sem", "n", "excl")

    def __init__(self, name, excl=False):
        self.name = name
        self.excl = excl
        self.w = None
        self.r = {}
        self.sem = None
        self.n = 0


class Kern:
    def __init__(self, nc, es):
        self.nc = nc
        self.es = es
        self.E = {"pe": nc.tensor, "act": nc.scalar, "dve": nc.vector, "pool": nc.gpsimd, "sp": nc.sync}
        self.sem = {e: es.enter_context(nc.semaphore("s_" + e)) for e in self.E}
        self.cnt = {e: 0 for e in self.E}
        self.seen = {e: {} for e in self.E}
        self.dmasems = []
        self.nsem = 0

    def _wait(self, eng, deps):
        need = {}
        for sem, val in deps:
            cur = need.get(sem.num)
            if cur is None or cur[1] < val:
                need[sem.num] = (sem, val)
        for num, (sem, val) in need.items():
            if self.seen[eng].get(num, 0) >= val:
                continue
            if sem is self.sem[eng] and (eng == "pe" or not SAME_ENG_SYNC):
                continue
            self.E[eng].wait_ge(sem, val)
            self.seen[eng][num] = val

    @staticmethod
    def _deps(R, W):
        deps = []
        for b in R:
            if b.w is not None:
                deps.append(b.w)
        for b in W:
            if b.w is not None:
                deps.append(b.w)
            deps.extend(b.r.values())
        return deps

    @staticmethod
    def _mark(tok, R, W):
        for b in R:
            cur = b.r.get(tok[0].num)
            if cur is None or cur[1] < tok[1]:
                b.r[tok[0].num] = tok
        for b in W:
            b.w = tok
            b.r = {}

    def op(self, eng, fn, R=(), W=(), inc=True):
        if any(b.excl for b in R):
            W = list(W) + [b for b in R if b.excl]
            R = [b for b in R if not b.excl]
        self._wait(eng, self._deps(R, W))
        ins = fn()
        tok = (self.sem[eng], self.cnt[eng] + 1)
        if inc:
            ins.then_inc(self.sem[eng], 1)
            self.cnt[eng] += 1
        self._mark(tok, R, W)
        return tok

    def _dsem(self, q, sembuf):
        kind = "sw" if q == "pool" else "hw"
        if sembuf.sem is None:
            sembuf.sem = {}
        if kind not in sembuf.sem:
            sm = self.es.enter_context(self.nc.semaphore("d%d" % self.nsem))
            self.nsem += 1
            sembuf.sem[kind] = [sm, 0]
            self.dmasems.append(sembuf.sem[kind])
        ent = sembuf.sem[kind]
        ent[1] += 1
        return ent[0], 16 * ent[1]

    def dma(self, q, out, in_, sembuf, R=(), W=()):
        self._wait(q, self._deps(R, W))
        sm, val = self._dsem(q, sembuf)
        self.E[q].dma_start(out=out, in_=in_).then_inc(sm, 16)
        tok = (sm, val)
        self._mark(tok, R, W)
        return tok

    def dma_custom(self, q, fn, sembuf, R=(), W=()):
        self._wait(q, self._deps(R, W))
        sm, val = self._dsem(q, sembuf)
        fn().then_inc(sm, 16)
        tok = (sm, val)
        self._mark(tok, R, W)
        return tok

    def barrier(self, engines=("pe", "act", "dve", "pool", "sp")):
        toks = [(self.sem[e], self.cnt[e]) for e in self.E if self.cnt[e] > 0]
        toks += [(e[0], 16 * e[1]) for e in self.dmasems]
        for e in engines:
            self._wait(e, toks)


class StopBuild(Exception):
    pass


class CFG:
    def __init__(self, tpc=2048, npre=48, nch=4, pass_tok=1024, ne=NE, stop=99, sparse=True):
        self.stop = stop
        self.sparse = sparse
        self.TPC = tpc
        self.NCHUNK = tpc // C
        self.NPRE = npre
        self.NCH = nch
        self.T = nch * C
        self.NBLK = tpc // self.T
        self.PT = pass_tok
        self.NPASS = tpc // pass_tok
        self.NE = ne


def _cpk_layout():
    names = [("ident", 128), ("maskT", 1024), ("XI", 1024), ("ZSR", 1024), ("GC", 8), ("freq4", 256),
             ("bgate", 32), ("gng", 16), ("sgug", 16), ("sgub", 16), ("wr", KT * 36), ("brrep", 36),
             ("trilT", 128), ("tgrid", 64), ("tokgrid", 16)]
    lay = {}
    o = 0
    for n, w in names:
        lay[n] = (o, w)
        o += w
    return lay, o


CPK, CPK_W = _cpk_layout()


def build(cfg):
    nc = bass.Bass("TRN2", target_bir_lowering=False)
    TPC, NPRE, NCH, T = cfg.TPC, cfg.NPRE, cfg.NCH, cfg.T
    NCHUNK = cfg.NCHUNK

    def din(name, shape, dt=F32):
        return nc.dram_tensor(name, list(shape), dt, kind="ExternalInput").ap()

    x_loc = din("x_loc", [TPC, D])
    x_pre = din("x_pre", [max(NPRE, 1) * C, D])
    pos_loc = din("pos_loc", [128, NCHUNK], I32)
    pos_pre = din("pos_pre", [128, max(NPRE, 1)], I32)
    cpk_d = din("cpk", [128, CPK_W])
    wsT_d = din("wsT", [128, 16 * 128])
    bsrep_d = din("bsrep", [128, 16 * 128])
    ln1g_d = din("ln1g", [128, D])
    ln1b_d = din("ln1b", [128, D])
    ln2g_d = din("ln2g", [128, D])
    ln2b_d = din("ln2b", [128, D])
    w_in = din("w_in", [D, IN_W])
    w_pr = din("w_proj_ret", [D, D])
    w_ps = din("w_proj_sgu", [D, D])
    w_out = din("w_out", [D, D])
    w1 = din("w1", [NE, D, DE])
    w3 = din("w3", [NE, D, DE])
    w2 = din("w2", [NE, DE, D])
    w1r = din("w1r", [NE * 128, 8192])
    w3r = din("w3r", [NE * 128, 8192])
    w2r = din("w2r", [NE * 128, 8192])
    y_out = nc.dram_tensor("y", [TPC, D], F32, kind="ExternalOutput").ap()
    X1 = nc.dram_tensor("x1_scr", [TPC, D], F32, kind="Internal").ap()
    X1T = nc.dram_tensor("x1t_scr", [128, KT, TPC], BF16, kind="Internal").ap()
    YS = nc.dram_tensor("ys_scr", [12288, D], F32, kind="Internal").ap()
    LIST = nc.dram_tensor("list_scr", [12288, 2], I32, kind="Internal").ap()
    b_X1 = Buf("X1")
    b_X1T = Buf("X1T")

    es = ExitStack()
    try:
        _build_body(nc, es, cfg, locals())
    except StopBuild:
        pass
    return nc


def _build_body(nc, es, cfg, L):
    globals_ = L
    (TPC, NPRE, NCH, T, NCHUNK) = (L['TPC'], L['NPRE'], L['NCH'], L['T'], L['NCHUNK'])
    x_loc, x_pre, pos_loc, pos_pre, cpk_d, wsT_d, bsrep_d = (L[k] for k in ('x_loc','x_pre','pos_loc','pos_pre','cpk_d','wsT_d','bsrep_d'))
    ln1g_d, ln1b_d, ln2g_d, ln2b_d, w_in, w_pr, w_ps, w_out, w1, w3, w2 = (L[k] for k in ('ln1g_d','ln1b_d','ln2g_d','ln2b_d','w_in','w_pr','w_ps','w_out','w1','w3','w2'))
    w1r, w3r, w2r = L['w1r'], L['w3r'], L['w2r']
    y_out, X1, X1T, b_X1, b_X1T, YS, LIST = (L[k] for k in ('y_out','X1','X1T','b_X1','b_X1T','YS','LIST'))
    with es:
        K = Kern(nc, es)

        def chk(n):
            if cfg.stop == n:
                K.barrier()
                raise StopBuild()
        RING_N = 3
        SB_TOTAL = 207 * 1024
        CP_B = 17664
        KEEP_B = 27 * 1024
        RING_B = RING_N * 16384
        ARENA_B = SB_TOTAL - CP_B - KEEP_B - RING_B
        sb = es.enter_context(nc.sbuf_tensor("sb", [128, SB_TOTAL], U8))
        cp = sb[:, 0:CPK_W * 4].bitcast(F32)
        keep = sb[:, CP_B:CP_B + KEEP_B]
        ringb = sb[:, CP_B + KEEP_B:CP_B + KEEP_B + RING_B]
        ring_v = [ringb[:, i * 16384:(i + 1) * 16384].bitcast(BF16) for i in range(RING_N)]
        arena = sb[:, CP_B + KEEP_B + RING_B:SB_TOTAL]
        ps = es.enter_context(nc.psum_tensor("ps", [128, 8, 512], F32))
        b_cp = Buf("cp")
        ring_b = [Buf("ring%d" % i) for i in range(RING_N)]
        bank_b = [Buf("bank%d" % i, excl=True) for i in range(8)]
        st = {"bank": 0}

        def pb():
            i = st["bank"] % 8
            st["bank"] += 1
            return bank_b[i], ps[:, i, :]

        def carve(base, off, shape, dt):
            esz = {F32: 4, BF16: 2, I32: 4, U8: 1}[dt]
            n = 1
            for s in shape[1:]:
                n *= s
            ap = base[:, off:off + n * esz]
            if dt != U8:
                ap = ap.bitcast(dt)
            if len(shape) == 3:
                ap = ap.rearrange("p (a b) -> p a b", a=shape[1])
            elif len(shape) == 4:
                ap = ap.rearrange("p (a b c) -> p a b c", a=shape[1], b=shape[2])
            return ap, off + n * esz

        def cpv(name):
            o, w = CPK[name]
            return cp[:, o:o + w]

        K.dma("sp", cp, cpk_d, b_cp, W=[b_cp])
        ident = cpv("ident")
        maskT = cpv("maskT").rearrange("p (h n) -> p h n", h=H)
        XI = cpv("XI").rearrange("p (h n) -> p h n", h=H)
        ZSR = cpv("ZSR").rearrange("p (h n) -> p h n", h=H)
        GC = cpv("GC")
        freq4 = cpv("freq4")
        bgate = cpv("bgate").rearrange("p (a b) -> p a b", a=2)
        gng = cpv("gng")
        sgug = cpv("sgug")
        sgub = cpv("sgub")
        wr = cpv("wr").rearrange("p (k c) -> p k c", k=KT)
        brrep = cpv("brrep")
        trilT = cpv("trilT")

        ko = 0
        S32, ko = carve(keep, ko, [128, H * DV], F32)
        Sbf, ko = carve(keep, ko, [128, H * DV], BF16)
        WsT, ko = carve(keep, ko, [128, 16, 128], BF16)
        B2, ko = carve(keep, ko, [128, 16, 128], F32)
        comb, ko = carve(keep, ko, [128, NCHUNK, NE], F32)
        ones32, ko = carve(keep, ko, [128, 128], F32)
        assert ko <= KEEP_B, ko
        b_S32 = [Buf("S32_%d" % h) for h in range(H)]
        b_Sbf = [Buf("Sbf_%d" % h) for h in range(H)]
        b_WsT, b_B2, b_ones = Buf("WsT"), Buf("B2"), Buf("ones")
        b_comb = [Buf("comb%d" % i) for i in range(NCHUNK)]

        slabs = []
        ringall = sb[:, CP_B + KEEP_B:CP_B + KEEP_B + 4 * 16384]
        ring_b.append(Buf("ring3"))

        def slot_ap(slot):
            return ringall[:, slot * 16384:(slot + 1) * 16384].bitcast(BF16)

        def v_k512(slot):
            return slot_ap(slot).rearrange("p (k c) -> p k c", k=KT)

        def v_k4(slot):
            return slot_ap(slot).rearrange("p (k c) -> p k c", k=4)

        def add_cols(wd, c0, width=512):
            src = wd[:, c0:c0 + width].rearrange("(k p) c -> p k c", p=128)
            slabs.append(dict(parts=[(lambda s: v_k512(s), src)]))

        def add_qk(hp):
            srcq = w_in[:, OFF_Q + hp * 256:OFF_Q + hp * 256 + 256].rearrange("(k p) c -> p k c", p=128)
            srck = w_in[:, OFF_K + hp * 256:OFF_K + hp * 256 + 256].rearrange("(k p) c -> p k c", p=128)
            slabs.append(dict(parts=[(lambda s: v_k512(s)[:, :, 0:256], srcq), (lambda s: v_k512(s)[:, :, 256:512], srck)]))

        for blk in range(cfg.NBLK):
            for hp in range(4):
                add_qk(hp)
                add_cols(w_in, OFF_V + hp * 512)
                add_cols(w_in, OFF_GR + hp * 512)
            for j in range(4):
                add_cols(w_in, OFF_VS + j * 512)
            for j in range(4):
                add_cols(w_in, OFF_U + j * 512)
            for j in range(4):
                add_cols(w_in, OFF_GA + j * 512)
                add_cols(w_in, OFF_GB + j * 512)
                add_cols(w_pr, j * 512)
                add_cols(w_ps, j * 512)
            for j in range(4):
                add_cols(w_out, j * 512)
        NSLAB_B = len(slabs)
        for i, sd in enumerate(slabs):
            sd["slot"] = i % 3
        for p_ in range(0 if cfg.sparse else cfg.NPASS):
            for e in range(cfg.NE):
                slabs.append(dict(parts=[(lambda s: v_k512(s), w1[e].rearrange("(k p) c -> p k c", p=128))]))
                slabs.append(dict(parts=[(lambda s: v_k512(s), w3[e].rearrange("(k p) c -> p k c", p=128))]))
                slabs.append(dict(parts=[(lambda s: v_k4(s), w2[e].rearrange("(k p) c -> p k c", p=128))]))
        for i in range(NSLAB_B, len(slabs)):
            slabs[i]["slot"] = (i - NSLAB_B) % 4
        lastin = {}
        for i, sd in enumerate(slabs):
            sd["prev"] = lastin.get(sd["slot"])
            lastin[sd["slot"]] = i
        sl = {"issued": 0, "next": 0, "limit": NSLAB_B, "rel": set()}

        def try_issue():
            while sl["issued"] < sl["limit"]:
                sd = slabs[sl["issued"]]
                if sd["prev"] is not None and sd["prev"] not in sl["rel"]:
                    break
                for vf, src in sd["parts"]:
                    K.dma("pool", vf(sd["slot"]), src, ring_b[sd["slot"]], W=[ring_b[sd["slot"]]])
                sl["issued"] += 1

        def acquire():
            i = sl["next"]
            sl["next"] += 1
            try_issue()
            assert sl["issued"] > i, (i, sl["issued"])
            return i, ring_b[slabs[i]["slot"]], slabs[i]["slot"]

        def release(i):
            sl["rel"].add(i)
            try_issue()

        def acquire_b():
            if sl["next"] > 0:
                sl["rel"].add(sl["next"] - 1)
            return acquire()

        def rstd_from_var(var_ap, out_ap, tmp_ap, bufs_r, bufs_w):
            K.op("act", lambda: nc.scalar.activation(out=tmp_ap, in_=var_ap, func=AF.Sqrt, bias=eps_col[:, 0:1], scale=1.0),
                 R=bufs_r, W=bufs_w)
            K.op("dve", lambda: nc.vector.reciprocal(out=out_ap, in_=tmp_ap), R=bufs_w, W=bufs_w)

        def trig_chunk(posf_col, ang, tmp, sin_o, cos_o, b_pos, b_t):
            K.op("dve", lambda: nc.vector.tensor_scalar(out=ang, in0=freq4, scalar1=posf_col, scalar2=None, op0=OP.mult),
                 R=[b_pos, b_cp], W=[b_t])
            K.op("dve", lambda: nc.vector.tensor_scalar(out=tmp, in0=ang, scalar1=1.0 / (2.0 * math.pi), scalar2=MAGIC,
                                                        op0=OP.mult, op1=OP.add), R=[b_t], W=[b_t])
            K.op("dve", lambda: nc.vector.tensor_scalar(out=tmp, in0=tmp, scalar1=-MAGIC, scalar2=None, op0=OP.add),
                 R=[b_t], W=[b_t])
            K.op("dve", lambda: nc.vector.scalar_tensor_tensor(out=ang, in0=tmp, scalar=-TWO_PI_HI, in1=ang,
                                                               op0=OP.mult, op1=OP.add), R=[b_t], W=[b_t])
            K.op("dve", lambda: nc.vector.scalar_tensor_tensor(out=ang, in0=tmp, scalar=-TWO_PI_LO, in1=ang,
                                                               op0=OP.mult, op1=OP.add), R=[b_t], W=[b_t])
            K.op("dve", lambda: nc.vector.tensor_scalar(out=ang, in0=ang, scalar1=-PI_SAFE, scalar2=PI_SAFE,
                                                        op0=OP.max, op1=OP.min), R=[b_t], W=[b_t])
            K.op("dve", lambda: nc.vector.scalar_tensor_tensor(out=tmp, in0=ang, scalar=-1.0, in1=ang, op0=OP.mult, op1=OP.max),
                 R=[b_t], W=[b_t])
            K.op("act", lambda: nc.scalar.activation(out=sin_o, in_=ang, func=AF.Sin), R=[b_t], W=[b_t])
            K.op("act", lambda: nc.scalar.activation(out=cos_o, in_=tmp, func=AF.Sin, bias=halfpi_col[:, 0:1], scale=-1.0),
                 R=[b_t], W=[b_t])

        def rotary(bank_ap, b_bank, cos4, sin4, b_t, out4, b_out, t1, t2, b_tmp):
            xv = bank_ap.rearrange("p (h t d) -> p h t d", h=4, t=2)
            ov = out4.rearrange("p h (t d) -> p h t d", t=2)
            c4 = cos4.rearrange("p (h d) -> p h d", h=4)
            s4 = sin4.rearrange("p (h d) -> p h d", h=4)
            x1, x2 = xv[:, :, 0, :], xv[:, :, 1, :]
            K.op("dve", lambda: nc.vector.tensor_tensor(out=t1, in0=x1, in1=c4, op=OP.mult), R=[b_bank, b_t], W=[b_tmp])
            K.op("dve", lambda: nc.vector.tensor_tensor(out=t2, in0=x2, in1=s4, op=OP.mult), R=[b_bank, b_t], W=[b_tmp])
            K.op("dve", lambda: nc.vector.tensor_tensor(out=ov[:, :, 0, :], in0=t1, in1=t2, op=OP.subtract),
                 R=[b_tmp], W=[b_out])
            K.op("dve", lambda: nc.vector.tensor_tensor(out=t1, in0=x2, in1=c4, op=OP.mult), R=[b_bank, b_t], W=[b_tmp])
            K.op("dve", lambda: nc.vector.tensor_tensor(out=t2, in0=x1, in1=s4, op=OP.mult), R=[b_bank, b_t], W=[b_tmp])
            K.op("dve", lambda: nc.vector.tensor_tensor(out=ov[:, :, 1, :], in0=t1, in1=t2, op=OP.add),
                 R=[b_tmp], W=[b_out])

        def transpose_to(src_ap, b_src, ntile, evac):
            for t0 in range(0, ntile, 4):
                n = min(4, ntile - t0)
                bb, bap = pb()
                for j in range(n):
                    K.op("pe", lambda j=j: nc.tensor.transpose(out=bap[:, j * 128:(j + 1) * 128],
                                                               in_=src_ap[:, (t0 + j) * 128:(t0 + j + 1) * 128],
                                                               identity=ident),
                         R=[b_src, b_cp], W=[bb], inc=(j == n - 1))
                evac(bap, bb, t0, n)

        def mm_group(bap, bb, pairs, R):
            n = len(pairs)
            for i, (l, r) in enumerate(pairs):
                K.op("pe", lambda l=l, r=r, i=i: nc.tensor.matmul(bap, lhsT=l, rhs=r, start=(i == 0), stop=(i == n - 1)),
                     R=R, W=[bb], inc=(i == n - 1))

        def layer_stats(src_chunks, b_src, mv_ap, stats_ap, b_stat):
            for j in range(4):
                K.op("dve", lambda j=j: nc.vector.bn_stats(out=stats_ap[:, j, :], in_=src_chunks[:, j * 512:(j + 1) * 512]),
                     R=[b_src], W=[b_stat])
            K.op("dve", lambda: nc.vector.bn_aggr(out=mv_ap, in_=stats_ap.rearrange("p a b -> p (a b)")), R=[b_stat], W=[b_stat])

        eps_col, ko2 = carve(keep, ko, [128, 1], F32)
        halfpi_col, ko2 = carve(keep, ko2, [128, 1], F32)
        assert ko2 <= KEEP_B
        b_cc = Buf("cc")
        K.op("dve", lambda: nc.vector.memset(eps_col, LN_EPS), W=[b_cc])
        K.op("dve", lambda: nc.vector.memset(halfpi_col, math.pi / 2.0), W=[b_cc])
        K.op("dve", lambda: nc.vector.memset(ones32, 1.0), W=[b_ones])
        for h in range(H):
            K.op("dve", lambda h=h: nc.vector.memset(S32[:, h * DV:(h + 1) * DV], 0.0), W=[b_S32[h]])

        ao = 0
        wsraw, ao = carve(arena, ao, [128, 16, 128], F32)
        bsrep, ao = carve(arena, ao, [128, 16, 128], F32)
        b_wsraw, b_bsrep = Buf("wsraw"), Buf("bsrep")
        K.dma("sp", wsraw, wsT_d.rearrange("p (g t) -> p g t", g=16), b_wsraw, W=[b_wsraw])
        K.dma("sp", bsrep, bsrep_d.rearrange("p (g t) -> p g t", g=16), b_bsrep, W=[b_bsrep])
        for g in range(16):
            K.op("dve", lambda g=g: nc.vector.tensor_tensor(out=wsraw[:, g, :], in0=wsraw[:, g, :], in1=trilT, op=OP.mult),
                 R=[b_cp], W=[b_wsraw])
        K.op("act", lambda: nc.scalar.copy(out=WsT, in_=wsraw), R=[b_wsraw], W=[b_WsT])
        for g in range(16):
            bb, bap = pb()
            K.op("pe", lambda g=g: nc.tensor.matmul(bap[:, 0:128], lhsT=ones32, rhs=wsraw[:, g, :], start=True, stop=True),
                 R=[b_ones, b_wsraw], W=[bb])
            K.op("dve", lambda g=g: nc.vector.scalar_tensor_tensor(out=B2[:, g, :], in0=bap[:, 0:128], scalar=sgub[:, g:g + 1],
                                                                   in1=bsrep[:, g, :], op0=OP.mult, op1=OP.add),
                 R=[bb, b_bsrep, b_cp], W=[b_B2])
        K.barrier()

        chk(0)
        if NPRE > 0:
            ao = 0
            wkv, ao = carve(arena, ao, [128, KT, 3072], BF16)
            assert ao <= ARENA_B
            b_wkv = Buf("wkv")
            K.dma("pool", wkv[:, :, 0:1024], w_in[:, OFF_K:OFF_K + 1024].rearrange("(k p) c -> p k c", p=128), b_wkv, W=[b_wkv])
            for j in range(4):
                K.dma("pool", wkv[:, :, 1024 + j * 512:1024 + (j + 1) * 512],
                      w_in[:, OFF_V + j * 512:OFF_V + (j + 1) * 512].rearrange("(k p) c -> p k c", p=128), b_wkv, W=[b_wkv])
            rbytes = ringb
            ro = 0
            xin2 = []
            for i in range(2):
                a_, ro = carve(rbytes, ro, [128, D], F32)
                xin2.append(a_)
            xTa = []
            for i in range(2):
                a_, ro = carve(rbytes, ro, [128, KT, 128], BF16)
                xTa.append(a_)
            pposi, ro = carve(rbytes, ro, [128, NPRE], I32)
            pposf, ro = carve(rbytes, ro, [128, NPRE], F32)
            tg = []
            for i in range(2):
                a1, ro = carve(rbytes, ro, [128, 256], F32)
                a2, ro = carve(rbytes, ro, [128, 256], F32)
                a3, ro = carve(rbytes, ro, [128, 256], F32)
                a4, ro = carve(rbytes, ro, [128, 256], F32)
                tg.append((a1, a2, a3, a4))
            krot, ro = carve(rbytes, ro, [128, 8, 128], F32)
            rt1, ro = carve(rbytes, ro, [128, 4, 64], F32)
            rt2, ro = carve(rbytes, ro, [128, 4, 64], F32)
            kz, ro = carve(rbytes, ro, [128, 8, 128], BF16)
            vbf, ro = carve(rbytes, ro, [128, H * DV], BF16)
            assert ro <= RING_N * 16384, ro
            b_xin = [Buf("xinA%d" % i) for i in range(2)]
            b_xT = [Buf("xTA%d" % i) for i in range(2)]
            b_pp = Buf("ppos")
            b_tg = [Buf("tgA%d" % i) for i in range(2)]
            b_krot, b_rt, b_kz, b_v = Buf("krot"), Buf("rt"), Buf("kz"), Buf("vA")
            K.dma("sp", pposi, pos_pre, b_pp, W=[b_pp])
            K.op("dve", lambda: nc.vector.tensor_copy(out=pposf, in_=pposi), R=[b_pp], W=[b_pp])
            for i in range(NPRE):
                xi_, bx = xin2[i % 2], b_xin[i % 2]
                xt_, bxt = xTa[i % 2], b_xT[i % 2]
                ang, tmp, sn, cs = tg[i % 2]
                btg = b_tg[i % 2]
                K.dma("sp", xi_, x_pre[i * C:(i + 1) * C, :], bx, W=[bx])
                trig_chunk(pposf[:, i:i + 1], ang, tmp, sn, cs, b_pp, btg)

                def ev_x(bap, bb, t0, n, xt_=xt_, bxt=bxt):
                    K.op("act", lambda: nc.scalar.copy(out=xt_[:, t0:t0 + n, :], in_=bap.rearrange("p (a b) -> p a b", a=4)),
                         R=[bb], W=[bxt])
                transpose_to(xi_, bx, KT, ev_x)
                kb = []
                for s_ in range(6):
                    bb, bap = pb()
                    mm_group(bap, bb, [(xt_[:, kt, :], wkv[:, kt, s_ * 512:(s_ + 1) * 512]) for kt in range(KT)], R=[bxt, b_wkv])
                    if s_ < 2:
                        rotary(bap, bb, cs, sn, btg, krot[:, s_ * 4:(s_ + 1) * 4, :], b_krot, rt1, rt2, b_rt)
                    else:
                        j = s_ - 2
                        K.op("act", lambda j=j, bap=bap: nc.scalar.copy(out=vbf[:, j * 512:(j + 1) * 512], in_=bap), R=[bb], W=[b_v])
                K.op("dve", lambda: nc.vector.tensor_tensor(out=kz, in0=krot, in1=ZSR, op=OP.mult), R=[b_krot, b_cp], W=[b_kz])
                for hp in range(4):
                    bb, bap = pb()
                    for hh in range(2):
                        h = hp * 2 + hh
                        K.op("pe", lambda h=h, hh=hh, bap=bap: nc.tensor.matmul(bap[:, hh * DV:(hh + 1) * DV], lhsT=kz[:, h, :],
                                                                                rhs=vbf[:, h * DV:(h + 1) * DV], start=True, stop=True),
                             R=[b_kz, b_v], W=[bb], inc=(hh == 1))
                    for hh in range(2):
                        h = hp * 2 + hh
                        K.op("dve", lambda h=h, hh=hh, bap=bap: nc.vector.scalar_tensor_tensor(
                            out=S32[:, h * DV:(h + 1) * DV], in0=S32[:, h * DV:(h + 1) * DV], scalar=GC[:, h:h + 1],
                            in1=bap[:, hh * DV:(hh + 1) * DV], op0=OP.mult, op1=OP.add), R=[bb, b_cp], W=[b_S32[h]])
            K.barrier()
        for h in range(H):
            K.op("act", lambda h=h: nc.scalar.copy(out=Sbf[:, h * DV:(h + 1) * DV], in_=S32[:, h * DV:(h + 1) * DV]),
                 R=[b_S32[h]], W=[b_Sbf[h]])

        chk(1)
        bo = 0
        xT, bo = carve(arena, bo, [128, KT, T], BF16)
        posi, bo = carve(arena, bo, [128, NCHUNK], I32)
        posf, bo = carve(arena, bo, [128, NCHUNK], F32)
        gst, bo = carve(arena, bo, [128, 4, 6], F32)
        AB0 = bo
        b_xTm, b_pos, b_gst = Buf("xTm"), Buf("posm"), Buf("gst")
        K.dma("sp", posi, pos_loc, b_pos, W=[b_pos])
        K.op("dve", lambda: nc.vector.tensor_copy(out=posf, in_=posi), R=[b_pos], W=[b_pos])

        for blk in range(cfg.NBLK):
            tok0 = blk * T
            o = AB0
            xin2 = []
            for i in range(2):
                a_, o = carve(arena, o, [128, D], F32)
                xin2.append(a_)
            b_xin = [Buf("xinB%d" % i) for i in range(2)]
            for c in range(NCH):
                xi_, bx = xin2[c % 2], b_xin[c % 2]
                K.dma("sp", xi_, x_loc[tok0 + c * C:tok0 + (c + 1) * C, :], bx, W=[bx])

                def ev_x(bap, bb, t0, n, c=c):
                    K.op("act", lambda: nc.scalar.copy(out=xT[:, t0:t0 + n, c * C:(c + 1) * C],
                                                      in_=bap.rearrange("p (a b) -> p a b", a=4)), R=[bb], W=[b_xTm])
                transpose_to(xi_, bx, KT, ev_x)
            K.barrier()
            chk(2)
            o = AB0
            zTa, o = carve(arena, o, [128, KT, T], BF16)
            b_zTa = Buf("zTa")
            trig = []
            b_trig = []
            for c in range(NCH):
                a1, o = carve(arena, o, [128, 256], F32)
                a2, o = carve(arena, o, [128, 256], F32)
                a3, o = carve(arena, o, [128, 256], F32)
                a4, o = carve(arena, o, [128, 256], F32)
                trig.append((a1, a2, a3, a4))
                b_trig.append(Buf("trig%d" % c))
            rot, o = carve(arena, o, [128, 4, 128], F32)
            rt1, o = carve(arena, o, [128, 4, 64], F32)
            rt2, o = carve(arena, o, [128, 4, 64], F32)
            qT, o = carve(arena, o, [128, 2, T], BF16)
            qxT, o = carve(arena, o, [128, 2, T], BF16)
            kT, o = carve(arena, o, [128, 2, T], BF16)
            kz, o = carve(arena, o, [128, NCH, 2, 128], BF16)
            vb, o = carve(arena, o, [128, NCH, 512], BF16)
            sg, o = carve(arena, o, [128, NCH, 512], F32)
            pm, o = carve(arena, o, [128, 2, 128], BF16)
            zp, o = carve(arena, o, [128, 512], F32)
            gmv, o = carve(arena, o, [128, 2, 2], F32)
            grs, o = carve(arena, o, [128, 2], F32)
            gtm, o = carve(arena, o, [128, 2], F32)
            gnb, o = carve(arena, o, [128, 2], F32)
            assert o <= ARENA_B, o
            b_rot, b_rt, b_qT, b_qxT, b_kT = Buf("rot"), Buf("rt"), Buf("qT"), Buf("qxT"), Buf("kT")
            b_kz = [Buf("kz%d" % c) for c in range(NCH)]
            b_vb = [Buf("vb%d" % c) for c in range(NCH)]
            b_sg = [Buf("sg%d" % c) for c in range(NCH)]
            b_pm, b_zp, b_gn = Buf("pm"), Buf("zp"), Buf("gn")
            for c in range(NCH):
                ang, tmp, sn, cs = trig[c]
                gi = (tok0 // C) + c
                trig_chunk(posf[:, gi:gi + 1], ang, tmp, sn, cs, b_pos, b_trig[c])
            for hp in range(4):
                h0 = hp * 2
                si_, rb, slot = acquire_b()
                wv = v_k512(slot)
                for c in range(NCH):
                    ct = slice(c * C, (c + 1) * C)
                    bb, bap = pb()
                    chk(200)
                    mm_group(bap, bb, [(xT[:, kt, ct], wv[:, kt, :]) for kt in range(KT)], R=[b_xTm, rb])
                    chk(201)
                    ang, tmp, sn, cs = trig[c]
                    rotary(bap, bb, cs, sn, b_trig[c], rot, b_rot, rt1, rt2, b_rt)
                    chk(202)
                    K.op("dve", lambda c=c: nc.vector.tensor_tensor(out=kz[:, c, :, :], in0=rot[:, 2:4, :], in1=ZSR[:, h0:h0 + 2, :],
                                                                    op=OP.mult), R=[b_rot, b_cp], W=[b_kz[c]])
                    chk(203)
                    b2, bap2 = pb()
                    for j in range(4):
                        K.op("pe", lambda j=j, bap2=bap2: nc.tensor.transpose(out=bap2[:, j * 128:(j + 1) * 128], in_=rot[:, j, :],
                                                                              identity=ident), R=[b_rot, b_cp], W=[b2], inc=(j == 3))
                    chk(204)
                    K.op("act", lambda ct=ct, bap2=bap2: nc.scalar.copy(out=qT[:, :, ct], in_=bap2[:, 0:256].rearrange("p (a b) -> p a b", a=2)),
                         R=[b2], W=[b_qT])
                    chk(205)
                    K.op("dve", lambda ct=ct, bap2=bap2: nc.vector.tensor_tensor(out=qxT[:, :, ct], in0=bap2[:, 0:256].rearrange("p (a b) -> p a b", a=2),
                                                                                 in1=XI[:, h0:h0 + 2, :], op=OP.mult), R=[b2, b_cp], W=[b_qxT])
                    K.op("act", lambda ct=ct, bap2=bap2: nc.scalar.copy(out=kT[:, :, ct], in_=bap2[:, 256:512].rearrange("p (a b) -> p a b", a=2)),
                         R=[b2], W=[b_kT])
                chk(21)
                si_, rb, slot = acquire_b()
                wv = v_k512(slot)
                for c in range(NCH):
                    ct = slice(c * C, (c + 1) * C)
                    bb, bap = pb()
                    mm_group(bap, bb, [(xT[:, kt, ct], wv[:, kt, :]) for kt in range(KT)], R=[b_xTm, rb])
                    K.op("act", lambda c=c, bap=bap: nc.scalar.copy(out=vb[:, c, :], in_=bap), R=[bb], W=[b_vb[c]])
                chk(22)
                si_, rb, slot = acquire_b()
                wv = v_k512(slot)
                for c in range(NCH):
                    ct = slice(c * C, (c + 1) * C)
                    bb, bap = pb()
                    mm_group(bap, bb, [(xT[:, kt, ct], wv[:, kt, :]) for kt in range(KT)], R=[b_xTm, rb])
                    K.op("act", lambda c=c, bap=bap: nc.scalar.activation(out=sg[:, c, :], in_=bap, func=AF.Silu), R=[bb], W=[b_sg[c]])
                chk(23)
                for c in range(NCH):
                    ct = slice(c * C, (c + 1) * C)
                    bs_, saps = pb()
                    for hh in range(2):
                        K.op("pe", lambda hh=hh, saps=saps, ct=ct: nc.tensor.matmul(saps[:, hh * 128:(hh + 1) * 128], lhsT=kT[:, hh, ct],
                                                                                   rhs=qT[:, hh, ct], start=True, stop=True),
                             R=[b_kT, b_qT], W=[bs_], inc=(hh == 1))
                    K.op("dve", lambda saps=saps: nc.vector.tensor_tensor(out=pm, in0=saps[:, 0:256].rearrange("p (a b) -> p a b", a=2),
                                                                          in1=maskT[:, h0:h0 + 2, :], op=OP.mult), R=[bs_, b_cp], W=[b_pm])
                    bo_, oap = pb()
                    for hh in range(2):
                        h = h0 + hh
                        K.op("pe", lambda hh=hh, oap=oap, c=c: nc.tensor.matmul(oap[:, hh * DV:(hh + 1) * DV], lhsT=pm[:, hh, :],
                                                                                rhs=vb[:, c, hh * DV:(hh + 1) * DV], start=True, stop=False),
                             R=[b_pm, b_vb[c]], W=[bo_], inc=False)
                        K.op("pe", lambda hh=hh, h=h, oap=oap, ct=ct: nc.tensor.matmul(oap[:, hh * DV:(hh + 1) * DV], lhsT=qxT[:, hh, ct],
                                                                                       rhs=Sbf[:, h * DV:(h + 1) * DV], start=False, stop=True),
                             R=[b_qxT, b_Sbf[h]], W=[bo_], inc=(hh == 1))
                    chk(24)
                    bt_, tap = pb()
                    for hh in range(2):
                        K.op("pe", lambda hh=hh, tap=tap, c=c: nc.tensor.matmul(tap[:, hh * DV:(hh + 1) * DV], lhsT=kz[:, c, hh, :],
                                                                                rhs=vb[:, c, hh * DV:(hh + 1) * DV], start=True, stop=True),
                             R=[b_kz[c], b_vb[c]], W=[bt_], inc=(hh == 1))
                    for hh in range(2):
                        h = h0 + hh
                        K.op("dve", lambda h=h, hh=hh, tap=tap: nc.vector.scalar_tensor_tensor(
                            out=S32[:, h * DV:(h + 1) * DV], in0=S32[:, h * DV:(h + 1) * DV], scalar=GC[:, h:h + 1],
                            in1=tap[:, hh * DV:(hh + 1) * DV], op0=OP.mult, op1=OP.add), R=[bt_, b_cp], W=[b_S32[h]])
                        K.op("act", lambda h=h: nc.scalar.copy(out=Sbf[:, h * DV:(h + 1) * DV], in_=S32[:, h * DV:(h + 1) * DV]),
                             R=[b_S32[h]], W=[b_Sbf[h]])
                    chk(25)
                    for hh in range(2):
                        K.op("dve", lambda hh=hh, oap=oap: nc.vector.bn_stats(out=gst[:, hh, :], in_=oap[:, hh * DV:(hh + 1) * DV]),
                             R=[bo_], W=[b_gst])
                        K.op("dve", lambda hh=hh: nc.vector.bn_aggr(out=gmv[:, hh, :], in_=gst[:, hh, :]), R=[b_gst], W=[b_gn])
                    rstd_from_var(gmv[:, :, 1], grs, gtm, [b_gn, b_cc], [b_gn])
                    for hh in range(2):
                        K.op("dve", lambda hh=hh, oap=oap: nc.vector.tensor_scalar(out=zp[:, hh * DV:(hh + 1) * DV], in0=oap[:, hh * DV:(hh + 1) * DV],
                                                                                   scalar1=gmv[:, hh, 0:1], scalar2=grs[:, hh:hh + 1],
                                                                                   op0=OP.subtract, op1=OP.mult), R=[bo_, b_gn], W=[b_zp])
                    K.op("dve", lambda c=c: nc.vector.tensor_tensor(out=zp, in0=zp, in1=sg[:, c, :], op=OP.mult), R=[b_sg[c]], W=[b_zp])

                    chk(26)

                    def ev_z(bap, bb, t0, n, ct=ct):
                        for j in range(n):
                            kt = hp * 4 + t0 + j
                            K.op("act", lambda j=j, kt=kt: nc.scalar.activation(out=zTa[:, kt, ct], in_=bap[:, j * 128:(j + 1) * 128],
                                                                                func=AF.Identity, scale=gng[:, kt:kt + 1]),
                                 R=[bb, b_cp], W=[b_zTa])
                    transpose_to(zp, b_zp, 4, ev_z)
            K.barrier()
            chk(3)
            o = AB0 + KT * T * 2
            gv, o = carve(arena, o, [128, NCH, D], F32)
            nb, o = carve(arena, o, [128, NCH, D], BF16)
            lmv, o = carve(arena, o, [128, NCH, 2], F32)
            lrs, o = carve(arena, o, [128, NCH], F32)
            ltm, o = carve(arena, o, [128, NCH], F32)
            assert o <= ARENA_B, o
            b_gv = [Buf("gv%d" % c) for c in range(NCH)]
            b_nb = [Buf("nb%d" % c) for c in range(NCH)]
            b_ln = Buf("ln")
            for j in range(4):
                si_, rb, slot = acquire_b()
                wv = v_k512(slot)
                for c in range(NCH):
                    ct = slice(c * C, (c + 1) * C)
                    bb, bap = pb()
                    mm_group(bap, bb, [(xT[:, kt, ct], wv[:, kt, :]) for kt in range(KT)], R=[b_xTm, rb])
                    K.op("act", lambda c=c, j=j, bap=bap: nc.scalar.activation(out=gv[:, c, j * 512:(j + 1) * 512], in_=bap, func=AF.Gelu),
                         R=[bb], W=[b_gv[c]])
            for c in range(NCH):
                layer_stats(gv[:, c, :], b_gv[c], lmv[:, c, :], gst, b_gst)
                K.op("dve", lambda c=c: nc.vector.tensor_copy(out=ltm[:, c:c + 1], in_=lmv[:, c, 1:2]), R=[b_gst], W=[b_ln])
            rstd_from_var(ltm, lrs, ltm, [b_ln, b_cc], [b_ln])
            for c in range(NCH):
                K.op("dve", lambda c=c: nc.vector.tensor_scalar(out=nb[:, c, :], in0=gv[:, c, :], scalar1=lmv[:, c, 0:1], scalar2=lrs[:, c:c + 1],
                                                                op0=OP.subtract, op1=OP.mult), R=[b_gv[c], b_ln, b_gst], W=[b_nb[c]])
            K.barrier()
            chk(4)
            o = AB0 + KT * T * 2 + NCH * D * 4
            o = AB0 + KT * T * 2
            ysT, o = carve(arena, o, [128, KT, T], BF16)
            guT, o = carve(arena, o, [128, 2, T], F32)
            stm, o = carve(arena, o, [128, 2, T], F32)
            assert o <= AB0 + KT * T * 2 + NCH * D * 4, o
            b_ysT = Buf("ysT")
            b_gu = [Buf("gu%d" % i) for i in range(2)]
            b_stm = [Buf("stm%d" % i) for i in range(2)]
            for j in range(4):
                si_, rb, slot = acquire_b()
                wv = v_k512(slot)
                for mt in range(4):
                    g = 4 * j + mt
                    bb, bap = pb()
                    mm_group(bap, bb, [(wv[:, kt, mt * 128:(mt + 1) * 128], xT[:, kt, :]) for kt in range(KT)], R=[b_xTm, rb])
                    i2 = g % 2
                    K.op("act", lambda i2=i2, bap=bap: nc.scalar.activation(out=guT[:, i2, :], in_=bap, func=AF.Gelu), R=[bb], W=[b_gu[i2]])
                    bm, map_ = pb()
                    for c in range(NCH):
                        K.op("pe", lambda c=c, g=g, map_=map_: nc.tensor.matmul(map_[:, c * C:(c + 1) * C], lhsT=nb[:, c, g * 128:(g + 1) * 128],
                                                                                rhs=WsT[:, g, :], start=True, stop=True),
                             R=[b_nb[c], b_WsT], W=[bm], inc=(c == NCH - 1))
                    for c in range(NCH):
                        K.op("dve", lambda c=c, g=g, i2=i2, map_=map_: nc.vector.scalar_tensor_tensor(
                            out=stm[:, i2, c * C:(c + 1) * C], in0=map_[:, c * C:(c + 1) * C], scalar=sgug[:, g:g + 1], in1=B2[:, g, :],
                            op0=OP.mult, op1=OP.add), R=[bm, b_B2, b_cp], W=[b_stm[i2]])
                    K.op("dve", lambda g=g, i2=i2: nc.vector.tensor_tensor(out=ysT[:, g, :], in0=stm[:, i2, :], in1=guT[:, i2, :], op=OP.mult),
                         R=[b_stm[i2], b_gu[i2]], W=[b_ysT])
            K.barrier()
            chk(5)
            o = AB0 + 2 * KT * T * 2
            mgT, o = carve(arena, o, [128, KT, T], BF16)
            sgA, o = carve(arena, o, [128, 4, T], F32)
            sgB, o = carve(arena, o, [128, 4, T], F32)
            mm_, o = carve(arena, o, [128, 4, T], F32)
            tt_, o = carve(arena, o, [128, T], F32)
            assert o <= ARENA_B, o
            b_mgT = Buf("mgT")
            b_sgA = [Buf("sgA%d" % i) for i in range(4)]
            b_sgB = [Buf("sgB%d" % i) for i in range(4)]
            b_mm = [Buf("mm%d" % i) for i in range(4)]
            b_tt = Buf("tt")
            for j in range(4):
                for br, sgt, bsg in ((0, sgA, b_sgA), (1, sgB, b_sgB)):
                    si_, rb, slot = acquire_b()
                    wv = v_k512(slot)
                    for mt in range(4):
                        bb, bap = pb()
                        mm_group(bap, bb, [(wv[:, kt, mt * 128:(mt + 1) * 128], xT[:, kt, :]) for kt in range(KT)], R=[b_xTm, rb])
                        K.op("act", lambda mt=mt, br=br, sgt=sgt, bap=bap: nc.scalar.activation(
                            out=sgt[:, mt, :], in_=bap, func=AF.Sigmoid, bias=bgate[:, br, 4 * j + mt:4 * j + mt + 1], scale=1.0),
                            R=[bb, b_cp], W=[bsg[mt]])
                si_, rb, slot = acquire_b()
                wv = v_k512(slot)
                for mt in range(4):
                    bb, bap = pb()
                    mm_group(bap, bb, [(wv[:, kt, mt * 128:(mt + 1) * 128], zTa[:, kt, :]) for kt in range(KT)], R=[b_zTa, rb])
                    K.op("dve", lambda mt=mt, bap=bap: nc.vector.tensor_tensor(out=mm_[:, mt, :], in0=bap, in1=sgA[:, mt, :], op=OP.mult),
                         R=[bb, b_sgA[mt]], W=[b_mm[mt]])
                si_, rb, slot = acquire_b()
                wv = v_k512(slot)
                for mt in range(4):
                    bb, bap = pb()
                    mm_group(bap, bb, [(wv[:, kt, mt * 128:(mt + 1) * 128], ysT[:, kt, :]) for kt in range(KT)], R=[b_ysT, rb])
                    K.op("dve", lambda mt=mt, bap=bap: nc.vector.tensor_tensor(out=tt_, in0=bap, in1=sgB[:, mt, :], op=OP.mult),
                         R=[bb, b_sgB[mt]], W=[b_tt])
                    K.op("dve", lambda mt=mt: nc.vector.tensor_tensor(out=mgT[:, 4 * j + mt, :], in0=tt_, in1=mm_[:, mt, :], op=OP.add),
                         R=[b_tt, b_mm[mt]], W=[b_mgT])
            K.barrier()
            chk(6)
            o = AB0
            x1 = []
            for c in range(NCH):
                a_, o = carve(arena, o, [128, D], F32)
                x1.append(a_)
            assert o <= AB0 + 2 * KT * T * 2
            o = AB0 + 3 * KT * T * 2
            g1, o = carve(arena, o, [128, D], F32)
            b1, o = carve(arena, o, [128, D], F32)
            x1Tf, o = carve(arena, o, [128, KT, 128], F32)
            x1Tb = []
            for i in range(2):
                a_, o = carve(arena, o, [128, KT, 128], BF16)
                x1Tb.append(a_)
            lg, o = carve(arena, o, [128, 40], F32)
            rsm, o = carve(arena, o, [128, 64], F32)
            assert o <= ARENA_B, o
            b_x1 = [Buf("x1_%d" % c) for c in range(NCH)]
            b_g1, b_b1, b_x1Tf = Buf("g1"), Buf("b1"), Buf("x1Tf")
            b_x1Tb = [Buf("x1Tb%d" % i) for i in range(2)]
            b_rt_ = Buf("route")
            K.dma("sp", g1, ln1g_d, b_g1, W=[b_g1])
            K.dma("sp", b1, ln1b_d, b_b1, W=[b_b1])
            for c in range(NCH):
                K.dma("sp", x1[c], x_loc[tok0 + c * C:tok0 + (c + 1) * C, :], b_x1[c], W=[b_x1[c]])
            for j in range(4):
                si_, rb, slot = acquire_b()
                wv = v_k512(slot)
                for c in range(NCH):
                    ct = slice(c * C, (c + 1) * C)
                    bb, bap = pb()
                    mm_group(bap, bb, [(mgT[:, kt, ct], wv[:, kt, :]) for kt in range(KT)], R=[b_mgT, rb])
                    K.op("dve", lambda c=c, j=j, bap=bap: nc.vector.scalar_tensor_tensor(
                        out=x1[c][:, j * 512:(j + 1) * 512], in0=x1[c][:, j * 512:(j + 1) * 512], scalar=DN_ALPHA, in1=bap,
                        op0=OP.mult, op1=OP.add), R=[bb], W=[b_x1[c]])
            for c in range(NCH):
                gi = (tok0 // C) + c
                ct = slice(c * C, (c + 1) * C)
                layer_stats(x1[c], b_x1[c], rsm[:, 0:2], gst, b_gst)
                rstd_from_var(rsm[:, 1:2], rsm[:, 2:3], rsm[:, 3:4], [b_gst, b_cc], [b_rt_])
                K.op("dve", lambda c=c: nc.vector.tensor_scalar(out=x1[c], in0=x1[c], scalar1=rsm[:, 0:1], scalar2=rsm[:, 2:3],
                                                                op0=OP.subtract, op1=OP.mult), R=[b_rt_, b_gst], W=[b_x1[c]])
                K.op("dve", lambda c=c: nc.vector.tensor_tensor(out=x1[c], in0=x1[c], in1=g1, op=OP.mult), R=[b_g1], W=[b_x1[c]])
                K.op("dve", lambda c=c: nc.vector.tensor_tensor(out=x1[c], in0=x1[c], in1=b1, op=OP.add), R=[b_b1], W=[b_x1[c]])
                K.dma("sp", X1[tok0 + c * C:tok0 + (c + 1) * C, :], x1[c], b_x1[c], R=[b_x1[c]], W=[b_X1])
                xb, bxb = x1Tb[c % 2], b_x1Tb[c % 2]

                def ev_x1(bap, bb, t0, n, xb=xb, bxb=bxb):
                    K.op("act", lambda: nc.scalar.copy(out=x1Tf[:, t0:t0 + n, :], in_=bap.rearrange("p (a b) -> p a b", a=4)), R=[bb], W=[b_x1Tf])
                    if not cfg.sparse:
                        K.op("dve", lambda: nc.vector.tensor_copy(out=xb[:, t0:t0 + n, :], in_=bap.rearrange("p (a b) -> p a b", a=4)), R=[bb], W=[bxb])
                transpose_to(x1[c], b_x1[c], KT, ev_x1)
                if not cfg.sparse:
                    K.dma("sp", X1T[:, :, tok0 + c * C:tok0 + (c + 1) * C], xb, bxb, R=[bxb], W=[b_X1T])
                bb, bap = pb()
                mm_group(bap[:, 0:36], bb, [(x1Tf[:, kt, :], wr[:, kt, :]) for kt in range(KT)], R=[b_x1Tf, b_cp])
                R_ = [b_rt_]
                K.op("dve", lambda bap=bap: nc.vector.tensor_tensor(out=lg[:, 0:36], in0=bap[:, 0:36], in1=brrep, op=OP.add),
                     R=[bb, b_cp], W=R_)
                gl = lg[:, 0:4]
                el = lg[:, 4:36].rearrange("p (g e) -> p g e", g=4)
                gmax, negm, oneh, sume, esel, top8 = rsm[:, 8:9], rsm[:, 9:10], rsm[:, 10:14], rsm[:, 14:15], rsm[:, 16:24], rsm[:, 24:32]
                ex4, pgrp, dd, ee, p1, p2 = rsm[:, 32:36], rsm[:, 36:37], rsm[:, 37:38], rsm[:, 38:39], rsm[:, 39:40], rsm[:, 40:41]
                m1, m2, w8 = rsm[:, 41:49], rsm[:, 49:57], rsm[:, 56:64]
                K.op("dve", lambda: nc.vector.tensor_reduce(out=gmax, in_=gl, axis=mybir.AxisListType.X, op=OP.max), R=R_, W=R_)
                K.op("dve", lambda: nc.vector.tensor_scalar(out=oneh, in0=gl, scalar1=gmax, scalar2=None, op0=OP.is_equal), R=R_, W=R_)
                K.op("dve", lambda: nc.vector.tensor_scalar(out=negm, in0=gmax, scalar1=-1.0, scalar2=None, op0=OP.mult), R=R_, W=R_)
                K.op("act", lambda: nc.scalar.activation(out=ex4, in_=gl, func=AF.Exp, bias=negm, scale=1.0), R=R_, W=R_)
                K.op("dve", lambda: nc.vector.tensor_reduce(out=sume, in_=ex4, axis=mybir.AxisListType.X, op=OP.add), R=R_, W=R_)
                K.op("dve", lambda: nc.vector.reciprocal(out=pgrp, in_=sume), R=R_, W=R_)
                K.op("dve", lambda: nc.vector.tensor_scalar(out=esel, in0=el[:, 0, :], scalar1=oneh[:, 0:1], scalar2=None, op0=OP.mult), R=R_, W=R_)
                for g in range(1, 4):
                    K.op("dve", lambda g=g: nc.vector.scalar_tensor_tensor(out=esel, in0=el[:, g, :], scalar=oneh[:, g:g + 1], in1=esel,
                                                                           op0=OP.mult, op1=OP.add), R=R_, W=R_)
                K.op("dve", lambda: nc.vector.max(out=top8, in_=esel), R=R_, W=R_)
                K.op("dve", lambda: nc.vector.tensor_scalar(out=m1, in0=esel, scalar1=top8[:, 0:1], scalar2=None, op0=OP.is_equal), R=R_, W=R_)
                K.op("dve", lambda: nc.vector.tensor_tensor(out=dd, in0=top8[:, 1:2], in1=top8[:, 0:1], op=OP.subtract), R=R_, W=R_)
                K.op("act", lambda: nc.scalar.activation(out=ee, in_=dd, func=AF.Exp), R=R_, W=R_)
                K.op("dve", lambda: nc.vector.tensor_scalar(out=p2, in0=ee, scalar1=1.0, scalar2=None, op0=OP.add), R=R_, W=R_)
                K.op("dve", lambda: nc.vector.reciprocal(out=p1, in_=p2), R=R_, W=R_)
                K.op("dve", lambda: nc.vector.tensor_tensor(out=p2, in0=ee, in1=p1, op=OP.mult), R=R_, W=R_)
                K.op("dve", lambda: nc.vector.tensor_tensor(out=p1, in0=p1, in1=pgrp, op=OP.mult), R=R_, W=R_)
                K.op("dve", lambda: nc.vector.tensor_tensor(out=p2, in0=p2, in1=pgrp, op=OP.mult), R=R_, W=R_)
                K.op("dve", lambda: nc.vector.tensor_scalar(out=w8, in0=esel, scalar1=top8[:, 1:2], scalar2=p2, op0=OP.is_equal, op1=OP.mult),
                     R=R_, W=R_)
                K.op("dve", lambda: nc.vector.scalar_tensor_tensor(out=w8, in0=m1, scalar=p1, in1=w8, op0=OP.mult, op1=OP.add), R=R_, W=R_)
                for g in range(4):
                    K.op("dve", lambda g=g, gi=gi: nc.vector.tensor_scalar(out=comb[:, gi, g * 8:(g + 1) * 8], in0=w8, scalar1=oneh[:, g:g + 1],
                                                                           scalar2=None, op0=OP.mult), R=R_, W=[b_comb[gi]])
            K.barrier()

        chk(7)
        if cfg.sparse:
            TS = 256
            NST = (2 * TPC) // TS + cfg.NE
            NTILE = NST * (TS // C)
            NSLOT = NST * TS
            assert NSLOT <= 12288 and NST <= 64
            BIG = 65536.0
            OOB = 4096
            sl["rel"].add(sl["next"] - 1)
            kb = keep
            o = 0
            NCE = NCHUNK * cfg.NE if cfg.NE == NE else NCHUNK * NE
            mask, o = carve(kb, o, [128, NCHUNK, NE], F32)
            POS, o = carve(kb, o, [128, NCHUNK, NE], F32)
            CNT, o = carve(kb, o, [128, NCHUNK, NE], F32)
            OFFC, o = carve(kb, o, [128, NCHUNK, NE], F32)
            TMPB, o = carve(kb, o, [128, NCHUNK, NE], F32)
            Umat, o = carve(kb, o, [128, 128], F32)
            sm, o = carve(kb, o, [128, 8, NE], F32)
            pA1, o = carve(kb, o, [128, NCHUNK], F32)
            pB1, o = carve(kb, o, [128, NCHUNK], F32)
            dupf, o = carve(kb, o, [128, NCHUNK], F32)
            pAi, o = carve(kb, o, [128, NCHUNK], I32)
            pBi, o = carve(kb, o, [128, NCHUNK], I32)
            REC, o = carve(kb, o, [128, NCHUNK, 2, 2], F32)
            t32, o = carve(kb, o, [128, NE], F32)
            eacc, o = carve(kb, o, [128, 64], F32)
            elist, o = carve(kb, o, [128, 64], I32)
            linit, o = carve(kb, o, [128, NSLOT // 128, 2], I32)
            cst, o = carve(kb, o, [128, 4, 6], F32)
            cmv, o = carve(kb, o, [128, 8], F32)
            assert o <= 24576, o
            tgrid, tokgrid = cpv("tgrid"), cpv("tokgrid")
            RECi = REC.bitcast(I32) if False else None
            rg_w = nc.gpsimd.to_reg(NE * 128 - 1)
            rg_x = nc.gpsimd.to_reg(TPC - 1)
            rg_s = nc.gpsimd.to_reg(NSLOT - 1)
            b_bk = Buf("bk")
            b_el, b_LIST, b_YS = Buf("elist"), Buf("LIST"), Buf("YS")
            Rb = [b_bk]
            maskf = mask.rearrange("p a b -> p (a b)")
            combf = comb.rearrange("p a b -> p (a b)")
            K.op("dve", lambda: nc.vector.tensor_single_scalar(out=maskf, in_=combf, scalar=0.0, op=OP.is_gt), R=b_comb, W=Rb)
            K.op("dve", lambda: nc.vector.tensor_tensor(out=Umat, in0=trilT, in1=ident, op=OP.subtract), R=[b_cp], W=Rb)
            bb, bap = pb()
            K.op("pe", lambda: nc.tensor.matmul(bap[:, 0:NCHUNK * NE], lhsT=ones32, rhs=maskf, start=True, stop=True), R=[b_ones, b_bk], W=[bb])
            K.op("dve", lambda: nc.vector.tensor_copy(out=CNT.rearrange("p a b -> p (a b)"), in_=bap[:, 0:NCHUNK * NE]), R=[bb], W=Rb)
            bb2, bap2 = pb()
            K.op("pe", lambda: nc.tensor.matmul(bap2[:, 0:NCHUNK * NE], lhsT=Umat, rhs=maskf, start=True, stop=True), R=[b_bk], W=[bb2])
            K.op("dve", lambda: nc.vector.tensor_copy(out=POS.rearrange("p a b -> p (a b)"), in_=bap2[:, 0:NCHUNK * NE]), R=[bb2], W=Rb)
            K.op("dve", lambda: nc.vector.memset(OFFC[:, 0, :], 0.0), W=Rb)
            for ch in range(1, NCHUNK):
                K.op("dve", lambda ch=ch: nc.vector.tensor_tensor(out=OFFC[:, ch, :], in0=OFFC[:, ch - 1, :], in1=CNT[:, ch - 1, :], op=OP.add), R=Rb, W=Rb)
            TOT, NT, END, BASEv, T1, T2 = (sm[:, i, :] for i in range(6))
            K.op("dve", lambda: nc.vector.tensor_tensor(out=TOT, in0=OFFC[:, NCHUNK - 1, :], in1=CNT[:, NCHUNK - 1, :], op=OP.add), R=Rb, W=Rb)
            K.op("dve", lambda: nc.vector.tensor_scalar(out=T1, in0=TOT, scalar1=1.0 / TS, scalar2=((TS - 1.0) / TS - 0.5 + 0.5 / TS),
                                                        op0=OP.mult, op1=OP.add), R=Rb, W=Rb)
            K.op("dve", lambda: nc.vector.tensor_scalar(out=T2, in0=T1, scalar1=MAGIC, scalar2=None, op0=OP.add), R=Rb, W=Rb)
            K.op("dve", lambda: nc.vector.tensor_scalar(out=NT, in0=T2, scalar1=-MAGIC, scalar2=None, op0=OP.add), R=Rb, W=Rb)
            K.op("dve", lambda: nc.vector.tensor_copy(out=END[:, 0:1], in_=NT[:, 0:1]), R=Rb, W=Rb)
            for e in range(1, NE):
                K.op("dve", lambda e=e: nc.vector.tensor_tensor(out=END[:, e:e + 1], in0=END[:, e - 1:e], in1=NT[:, e:e + 1], op=OP.add), R=Rb, W=Rb)
            K.op("dve", lambda: nc.vector.tensor_tensor(out=BASEv, in0=END, in1=NT, op=OP.subtract), R=Rb, W=Rb)
            K.op("dve", lambda: nc.vector.tensor_scalar(out=BASEv, in0=BASEv, scalar1=float(TS), scalar2=None, op0=OP.mult), R=Rb, W=Rb)
            for ch in range(NCHUNK):
                K.op("dve", lambda ch=ch: nc.vector.tensor_tensor(out=OFFC[:, ch, :], in0=OFFC[:, ch, :], in1=BASEv, op=OP.add), R=Rb, W=Rb)
            POSf, OFFf, TMPf = (a.rearrange("p a b -> p (a b)") for a in (POS, OFFC, TMPB))
            K.op("dve", lambda: nc.vector.tensor_tensor(out=POSf, in0=POSf, in1=OFFf, op=OP.add), R=Rb, W=Rb)
            K.op("dve", lambda: nc.vector.scalar_tensor_tensor(out=POSf, in0=POSf, scalar=1.0, in1=maskf, op0=OP.add, op1=OP.mult), R=Rb, W=Rb)
            K.op("dve", lambda: nc.vector.tensor_scalar(out=TMPf, in0=maskf, scalar1=-BIG, scalar2=BIG, op0=OP.mult, op1=OP.add), R=Rb, W=Rb)
            K.op("dve", lambda: nc.vector.tensor_tensor(out=TMPf, in0=TMPf, in1=POSf, op=OP.add), R=Rb, W=Rb)
            K.op("dve", lambda: nc.vector.tensor_reduce(out=pA1, in_=POS, axis=mybir.AxisListType.X, op=OP.max), R=Rb, W=Rb)
            K.op("dve", lambda: nc.vector.tensor_reduce(out=pB1, in_=TMPB, axis=mybir.AxisListType.X, op=OP.min), R=Rb, W=Rb)
            K.op("dve", lambda: nc.vector.tensor_tensor(out=dupf, in0=pA1, in1=pB1, op=OP.not_equal), R=Rb, W=Rb)
            RECi = REC.bitcast(I32)
            for ch in range(NCHUNK):
                for wi, pp in ((0, pA1), (1, pB1)):
                    K.op("dve", lambda ch=ch, pp=pp: nc.vector.scalar_tensor_tensor(out=t32, in0=POS[:, ch, :], scalar=pp[:, ch:ch + 1], in1=comb[:, ch, :],
                                                                                    op0=OP.is_equal, op1=OP.mult), R=Rb + [b_comb[ch]], W=Rb)
                    K.op("dve", lambda ch=ch, wi=wi: nc.vector.tensor_reduce(out=REC[:, ch, wi, 1:2], in_=t32, axis=mybir.AxisListType.X, op=OP.add), R=Rb, W=Rb)
                    K.op("dve", lambda ch=ch, wi=wi: nc.vector.tensor_copy(out=RECi[:, ch, wi, 0:1], in_=tokgrid[:, ch:ch + 1]), R=Rb + [b_cp], W=Rb)
            K.op("dve", lambda: nc.vector.tensor_scalar(out=pA1, in0=pA1, scalar1=-1.0, scalar2=None, op0=OP.add), R=Rb, W=Rb)
            K.op("dve", lambda: nc.vector.tensor_scalar(out=pB1, in0=pB1, scalar1=-1.0, scalar2=None, op0=OP.add), R=Rb, W=Rb)
            K.op("dve", lambda: nc.vector.tensor_copy(out=pAi, in_=pA1), R=Rb, W=Rb)
            K.op("dve", lambda: nc.vector.tensor_copy(out=pBi, in_=pB1), R=Rb, W=Rb)
            K.op("dve", lambda: nc.vector.memset(eacc, 0.0), W=Rb)
            for e in range(NE):
                K.op("dve", lambda e=e: nc.vector.scalar_tensor_tensor(out=eacc, in0=tgrid, scalar=END[:, e:e + 1], in1=eacc, op0=OP.is_ge, op1=OP.add),
                     R=Rb + [b_cp], W=Rb)
            K.op("dve", lambda: nc.vector.tensor_scalar(out=eacc, in0=eacc, scalar1=float(NE - 1), scalar2=None, op0=OP.min), R=Rb, W=Rb)
            eneq, _o = carve(kb, o, [128, 64], F32)
            assert _o <= 24576
            K.op("dve", lambda: nc.vector.memset(eneq, 1.0), W=Rb)
            K.op("dve", lambda: nc.vector.tensor_tensor(out=eneq[:, 2:64], in0=eacc[:, 2:64], in1=eacc[:, 0:62], op=OP.not_equal), R=Rb, W=Rb)
            K.op("dve", lambda: nc.vector.tensor_scalar(out=eacc, in0=eacc, scalar1=128.0, scalar2=tokgrid[:, 0:1], op0=OP.mult, op1=OP.add), R=Rb + [b_cp], W=Rb)
            K.op("dve", lambda: nc.vector.scalar_tensor_tensor(out=eacc, in0=eacc, scalar=-8192.0, in1=eneq, op0=OP.add, op1=OP.mult), R=Rb, W=Rb)
            K.op("dve", lambda: nc.vector.tensor_scalar(out=eacc, in0=eacc, scalar1=8192.0, scalar2=None, op0=OP.add), R=Rb, W=Rb)
            K.op("dve", lambda: nc.vector.tensor_copy(out=elist, in_=eacc), R=Rb, W=[b_el])
            eidx = elist
            chk(70)
            K.op("dve", lambda: nc.vector.memset(linit[:, :, 0:1], OOB), W=Rb)
            K.op("dve", lambda: nc.vector.memset(linit[:, :, 1:2], 0), W=Rb)
            K.dma("sp", LIST[0:NSLOT, :].rearrange("(p a) b -> p a b", p=128), linit, b_bk, R=Rb, W=[b_LIST])
            K._wait("pool", [b_LIST.w] + K._deps(Rb, []))
            b_sc = Buf("scat")
            for ch in range(NCHUNK):
                for wi, pi_ in ((0, pAi), (1, pBi)):
                    K.dma_custom("pool", lambda ch=ch, wi=wi, pi_=pi_: nc.gpsimd.indirect_dma_start(
                        out=LIST[:, :], out_offset=bass.IndirectOffsetOnAxis(ap=pi_[:, ch:ch + 1], axis=0),
                        in_=RECi[:, ch, wi, :], in_offset=None, bounds_check=rg_s, oob_is_err=False), b_sc, R=Rb, W=[])
            b_LIST.w = (b_sc.sem["sw"][0], 16 * b_sc.sem["sw"][1])

            chk(71)
            big = sb[:, CP_B + KEEP_B:SB_TOTAL]
            BIGB = SB_TOTAL - CP_B - KEEP_B
            ringc = [big[:, i * 16384:(i + 1) * 16384].bitcast(BF16) for i in range(6)]
            ringc_b = [Buf("ringc%d" % i) for i in range(6)]
            o = 6 * 16384
            xg, xgT, hd, hdT, yrow, idxt = [], [], [], [], [], []
            for i in range(2):
                a_, o = carve(big, o, [128, D], F32); xg.append(a_)
                a_, o = carve(big, o, [128, KT, 128], BF16); xgT.append(a_)
                a_, o = carve(big, o, [128, 512], F32); hd.append(a_)
                a_, o = carve(big, o, [128, 4, 128], BF16); hdT.append(a_)
                a_, o = carve(big, o, [128, D], F32); yrow.append(a_)
                a_, o = carve(big, o, [128, 2], I32); idxt.append(a_)
            ssl, o = carve(big, o, [128, 512], F32)
            lnrep, o = carve(big, o, [128, 2, D], F32)
            assert o <= BIGB, (o, BIGB)
            b_xg = [Buf("xg%d" % i) for i in range(2)]
            b_xgT = [Buf("xgT%d" % i) for i in range(2)]
            b_hd = [Buf("hd%d" % i) for i in range(2)]
            b_hdT = [Buf("hdT%d" % i) for i in range(2)]
            b_yrow = [Buf("yrow%d" % i) for i in range(2)]
            b_idx = [Buf("idx%d" % i) for i in range(2)]
            b_ssl, b_ln2 = Buf("sslS"), Buf("ln2S")
            for i in range(2):
                K.op("dve", lambda i=i: nc.vector.memset(xg[i], 0.0), W=[b_xg[i]])
            K.dma("sp", lnrep[:, 0, :], ln2g_d, b_ln2, W=[b_ln2])
            K.dma("sp", lnrep[:, 1, :], ln2b_d, b_ln2, W=[b_ln2])


            def w_loads(st_):
                wp = st_ % 2
                for k, wd in enumerate((w1r, w3r, w2r)):
                    sidx = wp * 3 + k
                    dst = ringc[sidx]
                    K.dma_custom("pool", lambda wd=wd, dst=dst: nc.gpsimd.indirect_dma_start(
                        out=dst, out_offset=None, in_=wd[:, :], in_offset=bass.IndirectOffsetOnAxis(ap=eidx[:, st_:st_ + 1], axis=0),
                        bounds_check=rg_w, oob_is_err=False), ringc_b[sidx], R=[b_el], W=[ringc_b[sidx]])

            def tile_loads(t):
                par = t % 2
                K.dma("sp", idxt[par], LIST[t * C:(t + 1) * C, :], b_idx[par], R=[b_LIST], W=[b_idx[par]])
                K.dma_custom("pool", lambda: nc.gpsimd.indirect_dma_start(
                    out=xg[par], out_offset=None, in_=X1[:, :], in_offset=bass.IndirectOffsetOnAxis(ap=idxt[par][:, 0:1], axis=0),
                    bounds_check=rg_x, oob_is_err=False), b_xg[par], R=[b_idx[par], b_X1], W=[b_xg[par]])

            def tile_compute(t):
                par = t % 2
                wpar = (t // (TS // C)) % 2
                wv1 = ringc[wpar * 3 + 0].rearrange("p (k c) -> p k c", k=KT)
                wv3 = ringc[wpar * 3 + 1].rearrange("p (k c) -> p k c", k=KT)
                wv2 = ringc[wpar * 3 + 2].rearrange("p (k c) -> p k c", k=4)
                rb1, rb3, rb2 = ringc_b[wpar * 3], ringc_b[wpar * 3 + 1], ringc_b[wpar * 3 + 2]
                cw = idxt[par].bitcast(F32)[:, 1:2]

                def ev_xg(bap, bb, t0, n):
                    eng = "act" if (t0 // 4) % 2 == 0 else "dve"
                    if eng == "act":
                        K.op("act", lambda: nc.scalar.copy(out=xgT[par][:, t0:t0 + n, :], in_=bap.rearrange("p (a b) -> p a b", a=4)), R=[bb], W=[b_xgT[par]])
                    else:
                        K.op("dve", lambda: nc.vector.tensor_copy(out=xgT[par][:, t0:t0 + n, :], in_=bap.rearrange("p (a b) -> p a b", a=4)), R=[bb], W=[b_xgT[par]])
                transpose_to(xg[par], b_xg[par], KT, ev_xg)
                b1_, a1 = pb()
                mm_group(a1, b1_, [(xgT[par][:, kt, :], wv1[:, kt, :]) for kt in range(KT)], R=[b_xgT[par], rb1])
                b3_, a3 = pb()
                mm_group(a3, b3_, [(xgT[par][:, kt, :], wv3[:, kt, :]) for kt in range(KT)], R=[b_xgT[par], rb3])
                K.op("act", lambda: nc.scalar.activation(out=ssl, in_=a1, func=AF.Silu), R=[b1_], W=[b_ssl])
                K.op("dve", lambda: nc.vector.scalar_tensor_tensor(out=hd[par], in0=a3, scalar=cw, in1=ssl, op0=OP.mult, op1=OP.mult),
                     R=[b3_, b_ssl, b_idx[par]], W=[b_hd[par]])

                def ev_hd(bap, bb, t0, n):
                    K.op("act", lambda: nc.scalar.copy(out=hdT[par], in_=bap.rearrange("p (a b) -> p a b", a=4)), R=[bb], W=[b_hdT[par]])
                transpose_to(hd[par], b_hd[par], 4, ev_hd)
                for j in range(4):
                    bb, bap = pb()
                    mm_group(bap, bb, [(hdT[par][:, ft, :], wv2[:, ft, j * 512:(j + 1) * 512]) for ft in range(4)], R=[b_hdT[par], rb2])
                    if j % 2 == 0:
                        K.op("act", lambda j=j, bap=bap: nc.scalar.copy(out=yrow[par][:, j * 512:(j + 1) * 512], in_=bap), R=[bb], W=[b_yrow[par]])
                    else:
                        K.op("dve", lambda j=j, bap=bap: nc.vector.tensor_copy(out=yrow[par][:, j * 512:(j + 1) * 512], in_=bap), R=[bb], W=[b_yrow[par]])
                K.dma("sp", YS[t * C:(t + 1) * C, :], yrow[par], b_yrow[par], R=[b_yrow[par]], W=[b_YS])

            SUB = TS // C
            w_loads(0)
            if NST > 1:
                w_loads(1)
            tile_loads(0)
            tile_loads(1)
            for t in range(NTILE):
                tile_compute(t)
                if t + 2 < NTILE:
                    tile_loads(t + 2)
                if t % SUB == SUB - 1 and (t // SUB) + 2 < NST:
                    w_loads(t // SUB + 2)
            chk(74)
            K.barrier()
            fin = []
            for i in range(2):
                fin.append((xg[i], yrow[i], b_xg[i], b_yrow[i]))
            x1c, o2 = carve(big, 0, [128, 2, D], F32)
            b_x1c = [ringc_b[0], ringc_b[1]]
            for ch in range(NCHUNK):
                i = ch % 2
                rA, rB, brA, brB = fin[i]
                K.dma_custom("pool", lambda rA=rA, ch=ch: nc.gpsimd.indirect_dma_start(
                    out=rA, out_offset=None, in_=YS[:, :], in_offset=bass.IndirectOffsetOnAxis(ap=pAi[:, ch:ch + 1], axis=0),
                    bounds_check=rg_s, oob_is_err=False), brA, R=[b_YS, b_bk], W=[brA])
                K.dma_custom("pool", lambda rB=rB, ch=ch: nc.gpsimd.indirect_dma_start(
                    out=rB, out_offset=None, in_=YS[:, :], in_offset=bass.IndirectOffsetOnAxis(ap=pBi[:, ch:ch + 1], axis=0),
                    bounds_check=rg_s, oob_is_err=False), brB, R=[b_YS, b_bk], W=[brB])
                xc = x1c[:, i, :]
                bxc = b_x1c[i]
                K.dma("sp", xc, X1[ch * C:(ch + 1) * C, :], bxc, R=[b_X1], W=[bxc])
                K.op("dve", lambda xc=xc, rA=rA: nc.vector.scalar_tensor_tensor(out=xc, in0=xc, scalar=DN_ALPHA, in1=rA, op0=OP.mult, op1=OP.add),
                     R=[brA], W=[bxc])
                K.op("dve", lambda xc=xc, rB=rB, ch=ch: nc.vector.scalar_tensor_tensor(out=xc, in0=rB, scalar=dupf[:, ch:ch + 1], in1=xc, op0=OP.mult, op1=OP.add),
                     R=[brB, b_bk], W=[bxc])
                layer_stats(xc, bxc, cmv[:, 0:2], cst, b_bk)
                K.op("act", lambda: nc.scalar.activation(out=cmv[:, 3:4], in_=cmv[:, 1:2], func=AF.Sqrt, bias=eps_col[:, 0:1], scale=1.0),
                     R=[b_bk, b_cc], W=[b_bk])
                K.op("dve", lambda: nc.vector.reciprocal(out=cmv[:, 2:3], in_=cmv[:, 3:4]), R=[b_bk], W=[b_bk])
                K.op("dve", lambda xc=xc: nc.vector.tensor_scalar(out=xc, in0=xc, scalar1=cmv[:, 0:1], scalar2=cmv[:, 2:3],
                                                                  op0=OP.subtract, op1=OP.mult), R=[b_bk], W=[bxc])
                K.op("dve", lambda xc=xc: nc.vector.tensor_tensor(out=xc, in0=xc, in1=lnrep[:, 0, :], op=OP.mult), R=[b_ln2], W=[bxc])
                K.op("dve", lambda xc=xc: nc.vector.tensor_tensor(out=xc, in0=xc, in1=lnrep[:, 1, :], op=OP.add), R=[b_ln2], W=[bxc])
                K.dma("sp", y_out[ch * C:(ch + 1) * C, :], xc, bxc, R=[bxc])
            K.barrier()
            return
        PT = cfg.PT
        NPC = PT // C
        NTL = PT // 512
        sl["rel"].add(sl["next"] - 1)
        sl["limit"] = len(slabs)
        r1 = sb[:, 0:CP_B + 24576]
        o = 0
        lnrep, o = carve(r1, o, [128, 2, D], F32)
        hdn, o = carve(r1, o, [128, 2, 4, 512], BF16)
        ssl, o = carve(r1, o, [128, 2, 512], F32)
        cst, o = carve(r1, o, [128, 4, 6], F32)
        cmv, o = carve(r1, o, [128, 8], F32)
        epsC, o = carve(r1, o, [128, 1], F32)
        assert o <= CP_B + 24576, o
        r2 = sb[:, CP_B + KEEP_B + 4 * 16384:SB_TOTAL]
        o = 0
        x1Tm, o = carve(r2, o, [128, KT, PT], BF16)
        yacc_t, o = carve(r2, o, [128, NPC, D], F32)
        assert o <= SB_TOTAL - (CP_B + KEEP_B + 4 * 16384), o
        b_x1Tm, b_ln2 = Buf("x1Tm"), Buf("ln2")
        b_hdn = [[Buf("hdn%d_%d" % (i, f)) for f in range(4)] for i in range(2)]
        b_ssl = [Buf("ssl%d" % i) for i in range(2)]
        b_ya = [Buf("ya%d" % c) for c in range(NPC)]
        b_cst = Buf("cst")
        b_ccC = Buf("ccC")
        K.op("dve", lambda: nc.vector.memset(epsC, LN_EPS), W=[b_ccC])
        K.dma("sp", lnrep[:, 0, :], ln2g_d, b_ln2, W=[b_ln2])
        K.dma("sp", lnrep[:, 1, :], ln2b_d, b_ln2, W=[b_ln2])
        for p_ in range(cfg.NPASS):
            t0p = p_ * PT
            K.dma("sp", x1Tm, X1T[:, :, t0p:t0p + PT], b_x1Tm, R=[b_X1T], W=[b_x1Tm])
            for c in range(NPC):
                K.dma("sp", yacc_t[:, c, :], X1[t0p + c * C:t0p + (c + 1) * C, :], b_ya[c], R=[b_X1], W=[b_ya[c]])
                K.op("act", lambda c=c: nc.scalar.mul(out=yacc_t[:, c, :], in_=yacc_t[:, c, :], mul=DN_ALPHA), W=[b_ya[c]])
            hcnt = 0
            for e in range(cfg.NE):
                i1, rb1, s1 = acquire()
                i3, rb3, s3 = acquire()
                wv1, wv3 = v_k512(s1), v_k512(s3)

                def do_h(tl, hb):
                    tt = slice(tl * 512, (tl + 1) * 512)
                    for ft in range(4):
                        b1_, a1 = pb()
                        mm_group(a1, b1_, [(wv1[:, kt, ft * 128:(ft + 1) * 128], x1Tm[:, kt, tt]) for kt in range(KT)], R=[b_x1Tm, rb1])
                        b3_, a3 = pb()
                        mm_group(a3, b3_, [(wv3[:, kt, ft * 128:(ft + 1) * 128], x1Tm[:, kt, tt]) for kt in range(KT)], R=[b_x1Tm, rb3])
                        i2 = ft % 2
                        K.op("act", lambda a1=a1, i2=i2: nc.scalar.activation(out=ssl[:, i2, :], in_=a1, func=AF.Silu), R=[b1_], W=[b_ssl[i2]])
                        K.op("dve", lambda a3=a3, i2=i2, ft=ft: nc.vector.tensor_tensor(out=hdn[:, hb, ft, :], in0=a3, in1=ssl[:, i2, :], op=OP.mult),
                             R=[b3_, b_ssl[i2]], W=[b_hdn[hb][ft]])

                def do_y(tl, hb, wv2, rb2):
                    for cc in range(4):
                        c = tl * 4 + cc
                        gi = t0p // C + c
                        for j in range(4):
                            bb, bap = pb()
                            mm_group(bap, bb, [(hdn[:, hb, ft, cc * C:(cc + 1) * C], wv2[:, ft, j * 512:(j + 1) * 512]) for ft in range(4)],
                                     R=[b_hdn[hb][0], b_hdn[hb][1], b_hdn[hb][2], b_hdn[hb][3], rb2])
                            K.op("dve", lambda c=c, j=j, gi=gi, bap=bap: nc.vector.scalar_tensor_tensor(
                                out=yacc_t[:, c, j * 512:(j + 1) * 512], in0=bap, scalar=comb[:, gi, e:e + 1], in1=yacc_t[:, c, j * 512:(j + 1) * 512],
                                op0=OP.mult, op1=OP.add), R=[bb, b_comb[gi]], W=[b_ya[c]])
                do_h(0, hcnt % 2)
                i2_ = None
                for tl in range(NTL):
                    if tl + 1 < NTL:
                        do_h(tl + 1, (hcnt + 1) % 2)
                    else:
                        release(i1)
                        release(i3)
                    if i2_ is None:
                        i2_, rb2, s2 = acquire()
                        wv2 = v_k4(s2)
                    do_y(tl, hcnt % 2, wv2, rb2)
                    hcnt += 1
                release(i2_)
            for c in range(NPC):
                ya = yacc_t[:, c, :]
                layer_stats(ya, b_ya[c], cmv[:, 0:2], cst, b_cst)
                K.op("act", lambda: nc.scalar.activation(out=cmv[:, 3:4], in_=cmv[:, 1:2], func=AF.Sqrt, bias=epsC[:, 0:1], scale=1.0),
                     R=[b_cst, b_ccC], W=[b_cst])
                K.op("dve", lambda: nc.vector.reciprocal(out=cmv[:, 2:3], in_=cmv[:, 3:4]), R=[b_cst], W=[b_cst])
                K.op("dve", lambda ya=ya: nc.vector.tensor_scalar(out=ya, in0=ya, scalar1=cmv[:, 0:1], scalar2=cmv[:, 2:3],
                                                                  op0=OP.subtract, op1=OP.mult), R=[b_cst], W=[b_ya[c]])
                K.op("dve", lambda ya=ya: nc.vector.tensor_tensor(out=ya, in0=ya, in1=lnrep[:, 0, :], op=OP.mult), R=[b_ln2], W=[b_ya[c]])
                K.op("dve", lambda ya=ya: nc.vector.tensor_tensor(out=ya, in0=ya, in1=lnrep[:, 1, :], op=OP.add), R=[b_ln2], W=[b_ya[c]])
                K.dma("sp", y_out[t0p + c * C:t0p + (c + 1) * C, :], ya, b_ya[c], R=[b_ya[c]])
            K.barrier()
        K.barrier()


def module_constants():
    lay, w = CPK, CPK_W
    cp = np.zeros((128, w), np.float32)

    def put(name, arr):
        o, ww = lay[name]
        cp[:, o:o + ww] = np.asarray(arr, np.float32).reshape(128, ww)
    put("ident", np.eye(128, dtype=np.float32))
    hh = np.arange(H, dtype=np.float32)
    log_gamma = np.log1p(-np.exp2(-5.0 - hh)).astype(np.float32)
    idx = np.arange(C, dtype=np.float32)
    rel = idx[:, None] - idx[None, :]
    dm = np.where(rel >= 0, np.exp(log_gamma[:, None, None] * np.maximum(rel, 0.0)), 0.0)
    scale = np.float32(DK ** -0.5)
    put("maskT", (dm.transpose(2, 0, 1) * scale))
    xi = np.exp(log_gamma[:, None] * (idx + 1.0))
    put("XI", np.broadcast_to(xi[None], (128, H, C)))
    zeta = np.exp(log_gamma[:, None] * (C - 1.0 - idx))
    put("ZSR", np.broadcast_to((zeta.T * scale)[:, :, None], (128, H, DK)))
    put("GC", np.broadcast_to(np.exp(log_gamma * C)[None], (128, H)))
    half = DK // 2
    freq = (np.float32(10000.0) ** (-np.arange(half, dtype=np.float32) / np.float32(half))).astype(np.float32)
    put("freq4", np.broadcast_to(np.tile(freq, 4)[None], (128, 256)))
    s_ = np.arange(128)
    put("trilT", (s_[:, None] <= s_[None, :]).astype(np.float32))
    put("tgrid", np.broadcast_to(np.arange(64, dtype=np.float32)[None], (128, 64)))
    put("tokgrid", (np.arange(16, dtype=np.float32)[None, :] * 128 + np.arange(128, dtype=np.float32)[:, None]))
    return cp


def prep_inputs(inp, cfg, n_cores, cores_per_row):
    cpk0 = module_constants()
    f = lambda a: np.ascontiguousarray(np.asarray(a, np.float32))
    x = f(inp["x"])
    pos = np.asarray(inp["positions"]).astype(np.int32)
    lay = CPK

    def put(cp, name, arr):
        o, ww = lay[name]
        cp[:, o:o + ww] = np.asarray(arr, np.float32).reshape(128, ww)
    cp = cpk0.copy()
    bg = f(inp["b_gate"])[0]
    put(cp, "bgate", bg.reshape(2, KT, 128).transpose(2, 0, 1))
    put(cp, "gng", f(inp["ret_gn_g"])[0].reshape(KT, 128).T)
    put(cp, "sgug", f(inp["sgu_ln_g"])[0].reshape(KT, 128).T)
    put(cp, "sgub", f(inp["sgu_ln_b"])[0].reshape(KT, 128).T)
    wrr = np.concatenate([f(inp["w_group"])[0], f(inp["w_er"])[0]], axis=1)
    put(cp, "wr", wrr.reshape(KT, 128, 36).transpose(1, 0, 2))
    brr = np.concatenate([f(inp["b_group"])[0], f(inp["b_er"])[0]])
    put(cp, "brrep", np.broadcast_to(brr[None], (128, 36)))
    wsT = f(inp["sgu_w"])[0].transpose(2, 0, 1).reshape(128, 16 * 128)
    bsrep = np.ascontiguousarray(np.broadcast_to(f(inp["sgu_b"])[0][None], (128, 16, 128))).reshape(128, 2048)
    rep = lambda v: np.ascontiguousarray(np.broadcast_to(f(v)[0][None], (128, D)))
    shared = {
        "cpk": cp, "wsT": np.ascontiguousarray(wsT), "bsrep": bsrep,
        "ln1g": rep(inp["ln1_g"]), "ln1b": rep(inp["ln1_b"]), "ln2g": rep(inp["ln2_g"]), "ln2b": rep(inp["ln2_b"]),
        "w_in": f(inp["w_in"])[0], "w_proj_ret": f(inp["w_proj_ret"])[0], "w_proj_sgu": f(inp["w_proj_sgu"])[0],
        "w_out": f(inp["w_out"])[0], "w1": f(inp["w1"])[0], "w3": f(inp["w3"])[0], "w2": f(inp["w2"])[0],
    }
    shared["w1r"] = np.ascontiguousarray(shared["w1"].reshape(NE, KT, 128, DE).transpose(0, 2, 1, 3)).reshape(NE * 128, 8192)
    shared["w3r"] = np.ascontiguousarray(shared["w3"].reshape(NE, KT, 128, DE).transpose(0, 2, 1, 3)).reshape(NE * 128, 8192)
    shared["w2r"] = np.ascontiguousarray(shared["w2"].reshape(NE, 4, 128, D).transpose(0, 2, 1, 3)).reshape(NE * 128, 8192)
    maps = []
    TPC, NPRE = cfg.TPC, cfg.NPRE
    for c in range(n_cores):
        b, seg = c // cores_per_row, c % cores_per_row
        t0 = seg * TPC
        m = dict(shared)
        m["x_loc"] = np.ascontiguousarray(x[b, t0:t0 + TPC])
        npre_tok = max(NPRE, 1) * C
        xp = np.zeros((npre_tok, D), np.float32)
        pp = np.zeros((npre_tok,), np.int32)
        if t0 > 0:
            xp[npre_tok - t0:] = x[b, :t0]
            pp[npre_tok - t0:] = pos[b, :t0]
        m["x_pre"] = xp
        m["pos_pre"] = np.ascontiguousarray(pp.reshape(-1, 128).T)
        m["pos_loc"] = np.ascontiguousarray(pos[b, t0:t0 + TPC].reshape(-1, 128).T)
        maps.append(m)
    return maps


def run(inp, cfg, n_cores, cores_per_row, trace=False):
    nc = build(cfg)
    maps = prep_inputs(inp, cfg, n_cores, cores_per_row)
    res = run_bass_kernel_spmd(nc, maps, core_ids=list(range(n_cores)), trace=trace)
    B = n_cores // cores_per_row
    out = np.stack([np.concatenate([np.asarray(res.results[b * cores_per_row + s]["y"]) for s in range(cores_per_row)], axis=0)
                    for b in range(B)], axis=0)
    return out.astype(np.float32), res


def kernel(**inputs):
    cfg = CFG(tpc=2048, npre=48, nch=4, pass_tok=1024)
    out, _ = run(inputs, cfg, 8, 4)
    return out
```
